# Optimizing a Trainium2 kernel written in Bass

```python
import numpy as np
import jax
import jax.numpy as jnp
from jax import lax

D_MODEL = 1024
BATCH = 32
SEQ = 2048
DEPTH = 2

HEAD_DIM = 64
N_HEADS_TOTAL = D_MODEL // HEAD_DIM
NSA_HEADS = N_HEADS_TOTAL // 4
FOX_HEADS = (N_HEADS_TOTAL - NSA_HEADS) // 2
SB_HEADS = N_HEADS_TOTAL - NSA_HEADS - FOX_HEADS
NSA_KV_HEADS = 1
NSA_GROUP = NSA_HEADS // NSA_KV_HEADS
FOX_W = FOX_HEADS * HEAD_DIM
SB_W = SB_HEADS * HEAD_DIM
NSA_W = NSA_HEADS * HEAD_DIM
NSA_KV_W = NSA_KV_HEADS * HEAD_DIM
D_MIX = FOX_W + SB_W + NSA_W
CMP_BLOCK = 32
CMP_STRIDE = 16
SEL_BLOCK = 64
SEL_TOPK = 8
SEL_N_LOCAL = 2
WINDOW = 512
Q_BLOCK = 128
NORM_EPS = 1e-6
SPLIT_SIZES = (FOX_W, FOX_W, FOX_W, FOX_HEADS, FOX_W,
               SB_W, SB_W, SB_W, SB_W,
               NSA_W, NSA_KV_W, NSA_KV_W, NSA_KV_W, NSA_KV_W, NSA_KV_W, NSA_KV_W,
               3 * NSA_HEADS, NSA_W)
D_IN = sum(SPLIT_SIZES)

kernel_name = 'hybrid_fox_stickbreak_nsa_block'


def rms_norm(x, g):
    xf = x.astype(jnp.float32)
    ms = jnp.mean(xf * xf, axis=-1, keepdims=True)
    return (xf * lax.rsqrt(ms + NORM_EPS) * g.astype(jnp.float32)).astype(x.dtype)


def masked_softmax(logits, mask):
    logits = jnp.where(mask, logits, -jnp.inf)
    m = jnp.max(logits, axis=-1, keepdims=True)
    m = jnp.where(jnp.isfinite(m), m, 0.0)
    p = jnp.exp(logits - m)
    s = jnp.sum(p, axis=-1, keepdims=True)
    return p / jnp.where(s > 0, s, 1.0)


def alibi_slopes(n):
    return 2.0 ** (-8.0 * jnp.arange(1, n + 1, dtype=jnp.float32) / n)


def fox_attention(q, k, v, f_logit):
    B, S, H, dh = q.shape
    scale = dh ** -0.5
    c = jnp.cumsum(jax.nn.log_sigmoid(f_logit.astype(jnp.float32)), axis=1).transpose(0, 2, 1)
    kpos = jnp.arange(S)

    def block(i):
        qs = i * Q_BLOCK
        qb = lax.dynamic_slice_in_dim(q, qs, Q_BLOCK, axis=1)
        cq = lax.dynamic_slice_in_dim(c, qs, Q_BLOCK, axis=2)
        tpos = qs + jnp.arange(Q_BLOCK)
        s = (jnp.einsum('bqhd,bkhd->bhqk', qb, k).astype(jnp.float32) * scale
             + cq[..., None] - c[:, :, None, :])
        p = masked_softmax(s, kpos[None, :] <= tpos[:, None])
        return jnp.einsum('bhqk,bkhd->bqhd', p.astype(v.dtype), v)

    out = lax.map(block, jnp.arange(S // Q_BLOCK))
    return out.transpose(1, 0, 2, 3, 4).reshape(B, S, H, dh)


def stick_breaking_attention(q, k, v):
    B, S, H, dh = q.shape
    scale = dh ** -0.5
    kpos = jnp.arange(S)

    def block(i):
        qs = i * Q_BLOCK
        qb = lax.dynamic_slice_in_dim(q, qs, Q_BLOCK, axis=1)
        tpos = qs + jnp.arange(Q_BLOCK)
        z = jnp.einsum('bqhd,bkhd->bhqk', qb, k).astype(jnp.float32) * scale
        mask = kpos[None, :] < tpos[:, None]
        log_1m = jnp.where(mask, jax.nn.log_sigmoid(-z), 0.0)
        between = lax.cumsum(log_1m, axis=3, reverse=True) - log_1m
        a = jnp.where(mask, jnp.exp(jax.nn.log_sigmoid(z) + between), 0.0)
        return jnp.einsum('bhqk,bkhd->bqhd', a.astype(v.dtype), v)

    out = lax.map(block, jnp.arange(S // Q_BLOCK))
    return out.transpose(1, 0, 2, 3, 4).reshape(B, S, H, dh)


def compress_blocks(kv, pos_emb, w1, w2, idx):
    B = kv.shape[0]
    n_cmp = idx.shape[0]
    blk = kv[:, idx] + pos_emb[None, None, :, None, :]
    blk = blk.transpose(0, 1, 3, 2, 4).reshape(B, n_cmp, NSA_KV_HEADS, CMP_BLOCK * HEAD_DIM)
    hid = jax.nn.silu(jnp.einsum('bnhf,fe->bnhe', blk, w1))
    return jnp.einsum('bnhe,ed->bnhd', hid, w2)


def nsa_attention(q, k_cmp, v_cmp, k_slc, v_slc, k_win, v_win, gates,
                  pos_k, w1_k, w2_k, pos_v, w1_v, w2_v, slopes):
    B, S = q.shape[0], q.shape[1]
    Hkv, G, dh = NSA_KV_HEADS, NSA_GROUP, HEAD_DIM
    scale = dh ** -0.5
    dt = q.dtype
    n_cmp = (S - CMP_BLOCK) // CMP_STRIDE + 1
    n_sel = S // SEL_BLOCK
    k_top = min(SEL_TOPK, n_sel)
    idx = np.arange(n_cmp)[:, None] * CMP_STRIDE + np.arange(CMP_BLOCK)[None, :]
    cmp_end = jnp.asarray(idx[:, -1])
    cs, ce = idx[:, 0], idx[:, -1]
    ss = np.arange(n_sel) * SEL_BLOCK
    se = ss + SEL_BLOCK - 1
    cmp_to_sel = jnp.asarray(((cs[:, None] <= se[None, :]) & (ce[:, None] >= ss[None, :])).astype(np.float32))
    kc = compress_blocks(k_cmp, pos_k, w1_k, w2_k, idx)
    vc = compress_blocks(v_cmp, pos_v, w1_v, w2_v, idx)
    ksb = k_slc.reshape(B, n_sel, SEL_BLOCK, Hkv, dh).transpose(0, 3, 1, 2, 4)
    vsb = v_slc.reshape(B, n_sel, SEL_BLOCK, Hkv, dh).transpose(0, 3, 1, 2, 4)
    pad = ((0, 0), (WINDOW, 0), (0, 0), (0, 0))
    kwp = jnp.pad(k_win, pad)
    vwp = jnp.pad(v_win, pad)
    bi = jnp.arange(B)[:, None, None, None]
    hi = jnp.arange(Hkv)[None, :, None, None]
    blk_id = jnp.arange(n_sel)
    sl = slopes[:, :, None, None]

    def block(i):
        qs = i * Q_BLOCK
        qb = lax.dynamic_slice_in_dim(q, qs, Q_BLOCK, axis=1)
        tpos = qs + jnp.arange(Q_BLOCK)
        dist_c = (tpos[:, None] - cmp_end[None, :]).astype(jnp.float32)
        s_c = jnp.einsum('bqhgd,bnhd->bhgqn', qb, kc).astype(jnp.float32) * scale - sl * dist_c
        p_c = masked_softmax(s_c, dist_c >= 0)
        o_c = jnp.einsum('bhgqn,bnhd->bqhgd', p_c.astype(dt), vc)
        imp = jnp.einsum('bhgqn,nj->bhqj', p_c, cmp_to_sel)
        cur = tpos // SEL_BLOCK
        back = cur[:, None] - blk_id[None, :]
        valid = back >= 0
        forced = (blk_id[None, :] == 0) | (valid & (back < SEL_N_LOCAL))
        imp = jnp.where(forced, jnp.inf, jnp.where(valid, imp, -jnp.inf))
        _, sel = lax.top_k(imp, k_top)
        kg = ksb[bi, hi, sel].reshape(B, Hkv, Q_BLOCK, k_top * SEL_BLOCK, dh)
        vg = vsb[bi, hi, sel].reshape(B, Hkv, Q_BLOCK, k_top * SEL_BLOCK, dh)
        kp = (sel[..., None] * SEL_BLOCK + jnp.arange(SEL_BLOCK)).reshape(B, Hkv, Q_BLOCK, k_top * SEL_BLOCK)
        dist_s = (tpos[None, None, :, None] - kp).astype(jnp.float32)[:, :, None]
        qh = qb.transpose(0, 2, 3, 1, 4)
        s_s = (jnp.einsum('bhgqd,bhqkd->bhgqk', qh, kg).astype(jnp.float32) * scale
               - slopes[None, :, :, None, None] * dist_s)
        p_s = masked_softmax(s_s, dist_s >= 0)
        o_s = jnp.einsum('bhgqk,bhqkd->bqhgd', p_s.astype(dt), vg)
        kw = lax.dynamic_slice_in_dim(kwp, qs, WINDOW + Q_BLOCK, axis=1)
        vw = lax.dynamic_slice_in_dim(vwp, qs, WINDOW + Q_BLOCK, axis=1)
        wpos = qs - WINDOW + jnp.arange(WINDOW + Q_BLOCK)
        dist_w = tpos[:, None] - wpos[None, :]
        mask_w = (wpos[None, :] >= 0) & (dist_w >= 0) & (dist_w < WINDOW)
        s_w = (jnp.einsum('bqhgd,bkhd->bhgqk', qb, kw).astype(jnp.float32) * scale
               - sl * dist_w.astype(jnp.float32))
        p_w = masked_softmax(s_w, mask_w)
        o_w = jnp.einsum('bhgqk,bkhd->bqhgd', p_w.astype(dt), vw)
        g = lax.dynamic_slice_in_dim(gates, qs, Q_BLOCK, axis=1).astype(dt)
        return o_c * g[..., 0:1] + o_s * g[..., 1:2] + o_w * g[..., 2:3]

    out = lax.map(block, jnp.arange(S // Q_BLOCK))
    return out.transpose(1, 0, 2, 3, 4, 5).reshape(B, S, NSA_W)


def setup_inputs(seed: int = 0) -> dict:
    key = jax.random.key(seed)
    ks = jax.random.split(key, 14)
    f32 = jnp.float32
    lf = CMP_BLOCK * HEAD_DIM
    x = jax.random.normal(ks[0], (BATCH, SEQ, D_MODEL), f32)
    norm_g = 1.0 + 0.01 * jax.random.normal(ks[1], (DEPTH, D_MODEL), f32)
    w_in = jax.random.normal(ks[2], (DEPTH, D_MODEL, D_IN), f32) * D_MODEL ** -0.5
    b_f = 2.0 + 0.1 * jax.random.normal(ks[3], (DEPTH, FOX_HEADS), f32)
    cmp_pos_k = 0.02 * jax.random.normal(ks[4], (DEPTH, CMP_BLOCK, HEAD_DIM), f32)
    cmp_w1_k = jax.random.normal(ks[5], (DEPTH, lf, HEAD_DIM), f32) * lf ** -0.5
    cmp_w2_k = jax.random.normal(ks[6], (DEPTH, HEAD_DIM, HEAD_DIM), f32) * HEAD_DIM ** -0.5
    cmp_pos_v = 0.02 * jax.random.normal(ks[7], (DEPTH, CMP_BLOCK, HEAD_DIM), f32)
    cmp_w1_v = jax.random.normal(ks[8], (DEPTH, lf, HEAD_DIM), f32) * lf ** -0.5
    cmp_w2_v = jax.random.normal(ks[9], (DEPTH, HEAD_DIM, HEAD_DIM), f32) * HEAD_DIM ** -0.5
    w_out = jax.random.normal(ks[10], (DEPTH, D_MIX, D_MODEL), f32) * D_MIX ** -0.5
    final_g = 1.0 + 0.01 * jax.random.normal(ks[11], (D_MODEL,), f32)
    return {'x': x, 'norm_g': norm_g, 'w_in': w_in, 'b_f': b_f,
            'cmp_pos_k': cmp_pos_k, 'cmp_w1_k': cmp_w1_k, 'cmp_w2_k': cmp_w2_k,
            'cmp_pos_v': cmp_pos_v, 'cmp_w1_v': cmp_w1_v, 'cmp_w2_v': cmp_w2_v,
            'w_out': w_out, 'final_g': final_g}


def reference(x, norm_g, w_in, b_f, cmp_pos_k, cmp_w1_k, cmp_w2_k,
              cmp_pos_v, cmp_w1_v, cmp_w2_v, w_out, final_g):
    B, S, _ = x.shape
    offsets = np.cumsum(SPLIT_SIZES)[:-1].tolist()
    slopes = alibi_slopes(NSA_HEADS).reshape(NSA_KV_HEADS, NSA_GROUP)

    def heads(t, n):
        return t.reshape(B, S, n, HEAD_DIM)

    def kv_heads(t):
        return t.reshape(B, S, NSA_KV_HEADS, HEAD_DIM)

    for l in range(DEPTH):
        h = rms_norm(x, norm_g[l])
        proj = jnp.einsum('bsd,de->bse', h, w_in[l])
        (fq, fk, fv, ff, fz, sq, sk, sv, sz,
         nq, nkc, nvc, nks, nvs, nkw, nvw, ng, nz) = jnp.split(proj, offsets, axis=-1)
        o_fox = fox_attention(heads(fq, FOX_HEADS), heads(fk, FOX_HEADS), heads(fv, FOX_HEADS),
                              ff + b_f[l]).reshape(B, S, FOX_W) * jax.nn.silu(fz)
        o_sb = stick_breaking_attention(heads(sq, SB_HEADS), heads(sk, SB_HEADS),
                                        heads(sv, SB_HEADS)).reshape(B, S, SB_W) * jax.nn.silu(sz)
        gates = jax.nn.sigmoid(ng.astype(jnp.float32)).reshape(B, S, NSA_KV_HEADS, NSA_GROUP, 3)
        o_nsa = nsa_attention(nq.reshape(B, S, NSA_KV_HEADS, NSA_GROUP, HEAD_DIM),
                              kv_heads(nkc), kv_heads(nvc), kv_heads(nks), kv_heads(nvs),
                              kv_heads(nkw), kv_heads(nvw), gates,
                              cmp_pos_k[l], cmp_w1_k[l], cmp_w2_k[l],
                              cmp_pos_v[l], cmp_w1_v[l], cmp_w2_v[l], slopes) * jax.nn.silu(nz)
        mixed = jnp.concatenate([o_fox, o_sb, o_nsa], axis=-1)
        x = x + jnp.einsum('bse,ed->bsd', mixed, w_out[l])
    return rms_norm(x, final_g)
```

```python
import os
import numpy as np
import ml_dtypes
from contextlib import ExitStack
import concourse.bass as bass
import concourse.mybir as mybir
from concourse.bass_utils import run_bass_kernel_spmd

F32 = mybir.dt.float32
BF16 = mybir.dt.bfloat16
AF = mybir.ActivationFunctionType
ALU = mybir.AluOpType

S_LEN = 2048
D = 1024
NCH = 16
D_IN = 3986
NEG = -30000.0
EPS = 1e-6
O_FQ, O_FK, O_FV, O_FF, O_FZ = 0, 384, 768, 1152, 1158
O_SQ, O_SK, O_SV, O_SZ = 1542, 1926, 2310, 2694
O_NQ, O_NKC, O_NVC, O_NKS, O_NVS, O_NKW, O_NVW, O_NG, O_NZ = 3078, 3334, 3398, 3462, 3526, 3590, 3654, 3718, 3730

ENGS = ("sp", "act", "pe", "dve", "pool")
PSUM_KEYS = frozenset(["PA0", "PA1", "PB0", "PB1", "PC", "PD", "PT1", "PT2"])
N_DMA_SEMS = 8


class Sched:
    def __init__(self):
        self.ops = []
        self.barriers = []

    def op(self, eng, fn, reads=(), writes=(), dma=False):
        self.ops.append(dict(eng=eng, fn=fn, reads=tuple(reads), writes=tuple(writes), dma=dma))

    def barrier(self):
        self.ops.append(dict(pseudo="barrier"))

    def mark(self):
        self.ops.append(dict(pseudo="mark"))

    def need(self, tag):
        self.ops.append(dict(pseudo="need", tag=tag))

    def done(self, tag):
        self.ops.append(dict(pseudo="done", tag=tag))

    def capture_begin(self):
        self._saved = self.ops
        self.ops = []

    def capture_end(self):
        cap, self.ops = self.ops, self._saved
        return cap

    @staticmethod
    def merge(a, b):
        def segs(lst):
            out, cur = [], []
            for o in lst:
                if o.get("pseudo") == "mark":
                    if cur:
                        out.append(cur)
                    cur = []
                else:
                    cur.append(o)
            if cur:
                out.append(cur)
            return out
        sa, sb_ = segs(a), segs(b)
        res, ia, ib = [], 0, 0
        done = set()

        def emit_a():
            nonlocal ia
            for o in sa[ia]:
                if o.get("pseudo") == "done":
                    done.add(o["tag"])
            res.extend(sa[ia])
            ia += 1
        while ia < len(sa) or ib < len(sb_):
            if ib < len(sb_):
                needs = [o["tag"] for o in sb_[ib] if o.get("pseudo") == "need" and o["tag"] not in done]
                if needs and ia < len(sa):
                    emit_a()
                    continue
            if ib >= len(sb_) or (ia < len(sa) and ia * len(sb_) <= ib * len(sa)):
                emit_a()
            else:
                res.extend(sb_[ib])
                ib += 1
        return res

    def finalize(self):
        ops, self.barriers = [], []
        for o in self.ops:
            p = o.get("pseudo")
            if p == "barrier":
                self.barriers.append(len(ops))
            elif p is not None:
                continue
            else:
                ops.append(o)
        self.ops = ops

    def analyze(self):
        self.finalize()
        ops = self.ops
        last_w, readers = {}, {}
        for i, o in enumerate(ops):
            deps = set()
            for r in o["reads"]:
                if r in last_w:
                    deps.add(last_w[r])
                if r in PSUM_KEYS:
                    for rd in readers.get(r, ()):
                        if ops[rd]["eng"] != o["eng"]:
                            deps.add(rd)
            for w in o["writes"]:
                if w in last_w:
                    deps.add(last_w[w])
                deps.update(readers.get(w, ()))
            deps.discard(i)
            o["deps"] = deps
            for r in o["reads"]:
                readers.setdefault(r, []).append(i)
            for w in o["writes"]:
                last_w[w] = i
                readers[w] = []
        for b in self.barriers:
            if b == 0 or b >= len(ops):
                continue
            lasts = {}
            dmas = []
            for i in range(b - 1, -1, -1):
                o = ops[i]
                if o["dma"]:
                    if len(dmas) < N_DMA_SEMS:
                        dmas.append(i)
                elif o["eng"] not in lasts:
                    lasts[o["eng"]] = i
                if len(lasts) == len(ENGS) and len(dmas) >= N_DMA_SEMS:
                    break
            seen = set()
            for i in range(b, len(ops)):
                e = ops[i]["eng"]
                if e in seen:
                    continue
                seen.add(e)
                ops[i]["deps"].update(lasts.values())
                ops[i]["deps"].update(dmas)
                if len(seen) == len(ENGS):
                    break
        observed = [False] * len(ops)
        for i, o in enumerate(ops):
            kept = set()
            for d in o["deps"]:
                od = ops[d]
                if od["eng"] == "pe" and o["eng"] == "pe" and not od["dma"] and not o["dma"]:
                    continue
                kept.add(d)
            o["deps"] = kept
        dma_idx = [i for i, o in enumerate(ops) if o["dma"]]
        for n, i in enumerate(dma_idx):
            ops[i]["dma_n"] = n
            if n >= N_DMA_SEMS:
                ops[i]["deps"].add(dma_idx[n - N_DMA_SEMS])
        for o in ops:
            for d in o["deps"]:
                observed[d] = True
        cnt = {e: 0 for e in ENGS}
        dcnt = [0] * N_DMA_SEMS
        for i, o in enumerate(ops):
            if o["dma"]:
                s = o["dma_n"] % N_DMA_SEMS
                dcnt[s] += 16
                o["sem"], o["val"], o["inc"] = ("dma", s), dcnt[s], True
            else:
                if observed[i]:
                    cnt[o["eng"]] += 1
                o["inc"] = observed[i]
                o["sem"], o["val"] = ("eng", o["eng"]), cnt[o["eng"]]

    def emit(self, nc, sems, dma_sems):
        ops = self.ops
        streams = {e: [i for i, o in enumerate(ops) if o["eng"] == e] for e in ENGS}

        def semof(key):
            return dma_sems[key[1]] if key[0] == "dma" else sems[key[1]]

        def run(ename, eng):
            seen = {}
            for i in streams[ename]:
                o = ops[i]
                need = {}
                for d in o["deps"]:
                    od = ops[d]
                    need[od["sem"]] = max(need.get(od["sem"], 0), od["val"])
                for k, v in need.items():
                    if seen.get(k, 0) >= v:
                        continue
                    eng.wait_ge(semof(k), v)
                    seen[k] = v
                ins = o["fn"](eng)
                if o["inc"]:
                    ins.then_inc(semof(o["sem"]), 16 if o["dma"] else 1)

        with nc.Block() as block:
            @block.sync
            def _(e):
                run("sp", e)

            @block.scalar
            def _(e):
                run("act", e)

            @block.tensor
            def _(e):
                run("pe", e)

            @block.vector
            def _(e):
                run("dve", e)

            @block.gpsimd
            def _(e):
                run("pool", e)


def _tables():
    bf = ml_dtypes.bfloat16
    t = {}
    t["ident"] = np.eye(128, dtype=np.float32).astype(bf)
    a = np.arange(128)[:, None]
    b = np.arange(128)[None, :]
    t["mdle4"] = np.tile(np.where(a <= b, 0.0, NEG), (1, 4)).astype(bf)
    t["mdfar4"] = np.tile(np.where(a > b, 0.0, NEG), (1, 4)).astype(bf)
    t["msb"] = np.where(b < a, 0.0, NEG).astype(bf)
    t["msbr"] = np.where(a + b > 127, 0.0, NEG).astype(bf)
    t["jm"] = np.ascontiguousarray(np.eye(128, dtype=np.float32)[::-1]).astype(bf)
    slopes = 2.0 ** (-8.0 * np.arange(1, 5) / 4.0)
    tpos = np.arange(S_LEN)
    nq = np.zeros((64, 4, S_LEN), np.float32)
    for g in range(4):
        nq[32, g] = slopes[g]
        nq[33, g] = slopes[g]
        nq[34, g] = -slopes[g] * 128.0 * (tpos // 128)
        nq[35, g] = -slopes[g] * (tpos % 128)
    t["nq_init"] = nq.reshape(64, 4 * S_LEN).astype(bf)
    kk = np.zeros((64, S_LEN), np.float32)
    kk[32] = tpos % 128
    kk[33] = 128.0 * (tpos // 128)
    kk[34] = 1.0
    kk[35] = 1.0
    t["nkw_init"] = kk.astype(bf)
    ks = kk.copy()
    for j in range(32):
        ks[j, 64 * j:64 * j + 64] = 1.0
    t["nks_init"] = ks.astype(bf)
    cpos = 16 * np.arange(127) + 31
    kc = np.zeros((64, 128), np.float32)
    kc[32, :127] = cpos % 128
    kc[33, :127] = 128.0 * (cpos // 128)
    kc[34, :127] = 1.0
    kc[35, :127] = 1.0
    t["kc_init"] = kc.astype(bf)
    bb = np.zeros((9, 256), np.float32)
    for k in range(9):
        bb[k, k + 126] = 1.0
    t["bigband"] = bb.astype(bf)
    mp = np.zeros((9, 128), np.float32)
    for r in range(9):
        mp[r, :] = np.where(np.arange(128) >= 16 * r - 1, 0.0, NEG)
    t["mpat"] = np.tile(mp, (1, 4)).astype(bf)
    cs_, ce_ = 16 * np.arange(127), 16 * np.arange(127) + 31
    ss_, se_ = 64 * np.arange(32), 64 * np.arange(32) + 63
    c2s = ((cs_[:, None] <= se_[None, :]) & (ce_[:, None] >= ss_[None, :])).astype(np.float32)
    vci = np.zeros((128, 33), np.float32)
    vci[:127, 0] = 1.0
    vci[:127, 1:] = c2s
    t["vc_init"] = vci.astype(bf)
    addm = np.zeros((128, 16, 32), np.float32)
    for i in range(16):
        cur = (128 * i + np.arange(128)) // 64
        back = cur[:, None] - np.arange(32)[None, :]
        valid = back >= 0
        forced = (np.arange(32)[None, :] == 0) | (valid & (back < 2))
        addm[:, i, :] = np.where(forced, 1e9, np.where(valid, 0.0, -1e9))
    t["addm"] = addm.reshape(128, 512)
    m3 = np.zeros((18, 3), np.float32)
    for p in range(18):
        m3[p, p % 3] = 1.0
    t["m3"] = m3
    return t


_TABLE_DT = {"addm": F32, "m3": F32}


def build_nc(n_seq=4, depth=2, tables=None, phases=("fox", "sb", "nsa", "out")):
    nc = bass.Bass("TRN2", target_bir_lowering=False)
    S = Sched()
    tables = tables or _tables()

    def dram(name, shape, dt, kind):
        return nc.dram_tensor(name, list(shape), dt, kind=kind).ap()

    x_d = dram("x", [n_seq, S_LEN, D], F32, "ExternalInput")
    y_d = dram("y", [n_seq, S_LEN, D], F32, "ExternalOutput")
    xs_d = dram("xs", [S_LEN, D], F32, "Internal")
    w_in_d = dram("w_in", [2, D, D_IN], F32, "ExternalInput")
    w_out_d = dram("w_out", [2, D, D], F32, "ExternalInput")
    normg_d = dram("normg_t", [2, 128, 8], F32, "ExternalInput")
    bf3_d = dram("bf3", [2, 18, 1], F32, "ExternalInput")
    fg_d = dram("fg_rep", [128, D], F32, "ExternalInput")
    posk_d = dram("posk_t", [2, 64, 32], F32, "ExternalInput")
    posv_d = dram("posv_t", [2, 64, 32], F32, "ExternalInput")
    w1k_d = dram("w1k_t", [2, 64, 2048], F32, "ExternalInput")
    w1v_d = dram("w1v_t", [2, 64, 2048], F32, "ExternalInput")
    w2k_d = dram("w2k", [2, 64, 64], F32, "ExternalInput")
    w2v_d = dram("w2v", [2, 64, 64], F32, "ExternalInput")
    tab_d = {k: dram("t_" + k, v.shape, _TABLE_DT.get(k, BF16), "ExternalInput") for k, v in tables.items()}

    es = ExitStack()
    with es:
        def sb(name, shape, dt):
            return es.enter_context(nc.sbuf_tensor(name, list(shape), dt))

        def ps(name, shape, dt):
            return es.enter_context(nc.psum_tensor(name, list(shape), dt))

        HT = sb("HT", [128, 8, S_LEN], BF16)
        M = sb("M", [128, NCH, D], BF16)
        WST = sb("WST", [128, 4, 512], F32)
        WB = [sb("WB%d" % i, [128, 8, 512], BF16) for i in range(2)]
        XT = [sb("XT%d" % i, [128, D], F32) for i in range(4)]
        XN = [sb("XN%d" % i, [128, D], BF16) for i in range(3)]
        PTL = [sb("PTL%d" % i, [128, 512], BF16) for i in range(4)]
        IDENT = sb("IDENT", [128, 128], BF16)
        MDLE4 = sb("MDLE4", [128, 512], BF16)
        MDFAR4 = sb("MDFAR4", [128, 512], BF16)
        MSB = sb("MSB", [128, 128], BF16)
        MSBR = sb("MSBR", [128, 128], BF16)
        JM = sb("JM", [128, 128], BF16)
        ADDM = sb("ADDM", [128, 16, 32], F32)
        ONESB = sb("ONESB", [128, 1024], BF16)
        BIGBAND = sb("BIGBAND", [9, 256], BF16)
        MPAT = sb("MPAT", [9, 512], BF16)
        M3 = sb("M3", [18, 3], F32)
        FG = sb("FG", [128, D], F32)
        G = sb("G", [128, 8], F32)
        SS = sb("SS", [128, 2 * NCH], F32)
        SQ_ = sb("SQ_", [128, 2 * NCH], F32)
        RS = sb("RS", [128, 2 * NCH], F32)
        VC = sb("VC", [128, 97], BF16)
        SMALL = sb("SMALL", [128, 256], F32)
        ARENA_BYTES = 73 * 1024
        ARENA = sb("ARENA", [128, ARENA_BYTES // 2], BF16)

        PA = ps("PA", [128, 1024], F32)
        PB = ps("PB", [128, 1024], F32)
        PC = ps("PC", [128, 512], F32)
        PD = ps("PD", [128, 512], F32)
        PT1 = ps("PT1", [128, 1024], BF16)
        PT2 = ps("PT2", [128, 1024], BF16)
        STB = [(PA[:, 0:512], "PA0"), (PA[:, 512:1024], "PA1"), (PB[:, 0:512], "PB0"), (PB[:, 512:1024], "PB1")]
        PTB = [(PT1, "PT1"), (PT2, "PT2")]
        OB = [(PC, "PC"), (PD, "PD")]

        sems = {e: es.enter_context(nc.semaphore("s_" + e)) for e in ENGS}
        dsems = [es.enter_context(nc.semaphore("d%d" % i)) for i in range(N_DMA_SEMS)]

        def carve(off, parts, nelem, dt):
            nb = nelem * (4 if dt == F32 else 2)
            assert off % 4 == 0 and off + nb <= ARENA_BYTES, (off, nb)
            ap = ARENA[0:parts, off // 2:(off + nb) // 2]
            return ap.bitcast(F32) if dt == F32 else ap

        def OP(eng, fn, r=(), w=(), dma=False):
            S.op(eng, fn, r, w, dma)

        def DMA(out, in_, r=(), w=()):
            OP("sp", lambda e, out=out, in_=in_: e.dma_start(out=out, in_=in_), r, w, dma=True)

        for nm, tile in [("ident", IDENT), ("mdle4", MDLE4), ("mdfar4", MDFAR4), ("msb", MSB), ("msbr", MSBR), ("jm", JM),
                         ("bigband", BIGBAND), ("mpat", MPAT), ("m3", M3)]:
            DMA(tile[:], tab_d[nm], w=[nm])
        DMA(ADDM[:].rearrange("p a b -> p (a b)"), tab_d["addm"], w=["addm"])
        DMA(FG[:], fg_d, w=["FG"])
        DMA(VC[:, 64:97], tab_d["vc_init"], w=["VCc"])
        OP("pool", lambda e: e.memset(ONESB[:], 1.0), w=["ONESB"])
        OP("pool", lambda e: e.memset(ARENA[:], 0.0), w=["ARENA0"])
        S.barrier()
        CONST = ["ident", "mdle4", "mdfar4", "msb", "bigband", "mpat", "m3", "addm", "FG", "VCc", "ONESB"]

        st_rr = [0]
        ptl_rr = [0]

        def next_st():
            st_rr[0] = (st_rr[0] + 1) % 4
            return STB[st_rr[0]]

        def next_ptl():
            ptl_rr[0] = (ptl_rr[0] + 1) % 4
            return PTL[ptl_rr[0]], "PTL%d" % ptl_rr[0]

        w_rr = [0]

        def wload(src3, specs, scale_g=True, buf=None, engines=("pool",)):
            if buf is None:
                w_rr[0] ^= 1
                buf = w_rr[0]
            wb = WB[buf]
            key = "WB%d" % buf
            tot = max(dc + n for (_, n, dc) in specs)
            for kh in range(2):
                stk = []
                for j, (sc, n, dc) in enumerate(specs):
                    DMA(WST[:, :, dc:dc + n], src3[:, 4 * kh:4 * kh + 4, sc:sc + n], w=[("WST", j)])
                    stk.append(("WST", j))
                for kk in range(4):
                    k = 4 * kh + kk
                    ce = engines[k % len(engines)]
                    if scale_g and ce == "act":
                        OP("act", lambda e, k=k, kk=kk, tot=tot, wb=wb: e.activation(
                            out=wb[:, k, 0:tot], in_=WST[:, kk, 0:tot], func=AF.Copy, scale=G[:, k:k + 1]),
                           r=stk + ["G"], w=[(key, k)])
                    elif scale_g and ce == "dve":
                        OP("dve", lambda e, k=k, kk=kk, tot=tot, wb=wb: e.tensor_scalar(
                            out=wb[:, k, 0:tot], in0=WST[:, kk, 0:tot], scalar1=G[:, k:k + 1], scalar2=None,
                            op0=ALU.mult), r=stk + ["G"], w=[(key, k)])
                    elif scale_g:
                        OP("pool", lambda e, k=k, kk=kk, tot=tot, wb=wb: e.tensor_scalar(
                            out=wb[:, k, 0:tot], in0=WST[:, kk, 0:tot], scalar1=G[:, k:k + 1], scalar2=0.0,
                            op0=ALU.mult, op1=ALU.add), r=stk + ["G"], w=[(key, k)])
                    else:
                        OP("pool", lambda e, k=k, kk=kk, tot=tot, wb=wb: e.tensor_copy(out=wb[:, k, 0:tot], in_=WST[:, kk, 0:tot]),
                           r=stk, w=[(key, k)])
            return wb, [(key, k) for k in range(8)]

        def proj_fm(wb, wkeys, c0, m, evac, banks=None, bank_rr=[0]):
            banks = banks or OB
            for tg in range(4):
                S.mark()
                bank_rr[0] += 1
                pb, pk = banks[bank_rr[0] % len(banks)]

                def f(e, tg=tg, pb=pb):
                    ins = None
                    for k in range(8):
                        ins = e.matmul(pb[0:m, 0:512], lhsT=wb[:, k, c0:c0 + m], rhs=HT[:, k, tg * 512:(tg + 1) * 512],
                                       start=(k == 0), stop=(k == 7))
                    return ins
                OP("pe", f, r=wkeys + [("HT", c) for c in range(tg * 4, tg * 4 + 4)], w=[pk])
                evac(tg, pb, pk)

        def proj_tm(wb, wkeys, c0, n, evac, banks=None, bank_rr=[0]):
            banks = banks or OB
            for c in range(NCH):
                if c % 2 == 0:
                    S.mark()
                bank_rr[0] += 1
                pb, pk = banks[bank_rr[0] % len(banks)]

                def f(e, c=c, pb=pb):
                    ins = None
                    for k in range(8):
                        ins = e.matmul(pb[:, 0:n], lhsT=HT[:, k, c * 128:(c + 1) * 128], rhs=wb[:, k, c0:c0 + n],
                                       start=(k == 0), stop=(k == 7))
                    return ins
                OP("pe", f, r=wkeys + [("HT", c)], w=[pk])
                evac(c, pb, pk)

        tp_rr = [0]

        def tpose(src_ap, src_keys, c):
            tp_rr[0] ^= 1
            pt, ptk = PTB[tp_rr[0]]

            def f(e, pt=pt):
                ins = None
                for k in range(8):
                    ins = e.transpose(out=pt[:, k * 128:(k + 1) * 128], in_=src_ap[:, k * 128:(k + 1) * 128], identity=IDENT[:])
                return ins
            OP("pe", f, r=list(src_keys) + ["ident"], w=[ptk])

            def evac():
                if True:
                    OP("act", lambda e, pt=pt, c=c: e.activation(out=HT[:, :, c * 128:(c + 1) * 128],
                                                                 in_=pt[:].rearrange("p (k t) -> p k t", k=8), func=AF.Copy),
                       r=[ptk], w=[("HT", c)])
                else:
                    OP("dve", lambda e, pt=pt, c=c: e.tensor_copy(out=HT[:, :, c * 128:(c + 1) * 128],
                                                                  in_=pt[:].rearrange("p (k t) -> p k t", k=8)),
                       r=[ptk], w=[("HT", c)])
            return evac

        def transpose_into_HT(src_ap, src_keys, c):
            tpose(src_ap, src_keys, c)()

        def stats_scale(xt, xtk, col, n):
            xn, xnk = XN[n % 2], "XN%d" % (n % 2)
            OP("dve", lambda e, xt=xt, col=col: e.scalar_tensor_tensor(
                out=XN[2][:], in0=xt[:], scalar=1.0, in1=xt[:], op0=ALU.mult, op1=ALU.mult,
                accum_out=SS[:, col:col + 1]), r=[xtk], w=["XJ", ("SS", col)])
            OP("act", lambda e, col=col: e.activation(out=SQ_[:, col:col + 1], in_=SS[:, col:col + 1], func=AF.Sqrt,
                                                      bias=EPS, scale=1.0 / D), r=[("SS", col)], w=[("SQ", col)])
            OP("dve", lambda e, col=col: e.reciprocal(out=RS[:, col:col + 1], in_=SQ_[:, col:col + 1]), r=[("SQ", col)], w=[("RS", col)])
            OP("act", lambda e, xt=xt, xn=xn, col=col: e.activation(out=xn[:], in_=xt[:], func=AF.Copy, scale=RS[:, col:col + 1]),
               r=[xtk, ("RS", col)], w=[xnk])
            return xn, xnk

        def run_stages(stages, lo, hi):
            offs = [o for _, o in stages]
            for k in range(lo + min(offs), hi + max(offs)):
                for fn, off in stages:
                    n = k - off
                    if lo <= n < hi:
                        fn(n)

        pref = {}

        def fox_specs(p):
            return [(O_FQ + 128 * p, 128, 0), (O_FK + 128 * p, 128, 128), (O_FV + 128 * p, 128, 256), (O_FZ + 128 * p, 128, 384)]

        def sb_specs(p):
            return [(O_SQ + 128 * p, 128, 0), (O_SK + 128 * p, 128, 128), (O_SV + 128 * p, 128, 256), (O_SZ + 128 * p, 128, 384)]

        def prefetch_next(l2):
            w3 = w_in_d[l2].rearrange("(k p) e -> p k e", p=128)
            DMA(G[:], normg_d[l2], w=["G"])
            pref["fox"] = wload(w3, fox_specs(0), buf=0, engines=("act", "dve", "pool"))
            pref["sb"] = wload(w3, sb_specs(0), buf=1, engines=("act", "dve", "pool"))

        def unit(s, l, last, do_phase_a, nxt):
            w_in3 = w_in_d[l].rearrange("(k p) e -> p k e", p=128)
            w_out3 = w_out_d[l].rearrange("(k p) e -> p k e", p=128)
            if not pref:
                DMA(G[:], normg_d[l], w=["G"])

            def xsrc(c):
                return (x_d[s] if l == 0 else xs_d)[c * 128:(c + 1) * 128, :]

            if do_phase_a:
                evs = {}

                def a_dma(n):
                    DMA(XT[n % 2][:], xsrc(n), r=[("XS", n)], w=["XT%d" % (n % 2)])

                def a_chain(n):
                    xn, xnk = stats_scale(XT[n % 2], "XT%d" % (n % 2), n, n)
                    evs[n] = tpose(xn, [xnk], n)

                def a_evac(n):
                    evs.pop(n)()
                run_stages([(a_chain, 0), (a_dma, -1), (a_evac, 1)], 0, NCH)

            HTALL = [("HT", c) for c in range(NCH)]

            FQ = carve(0, 128, 2048, BF16)
            FK = carve(4096, 128, 2048, BF16)
            VA = carve(8192, 128, NCH * 2 * 65, BF16).rearrange("p (c h d) -> p c h d", c=NCH, h=2)
            ZF = carve(12352, 128, NCH * 128, BF16).rearrange("p (c d) -> p c d", c=NCH)
            CT = carve(16448, 18, 2048, BF16)
            SBB = 22528
            CSC = SBB + 24576
            EF = carve(CSC, 18, 2048, F32)
            C32 = carve(CSC + 8192, 18, 2048, F32)
            HI = carve(CSC + 16384, 18, 2048, BF16)
            MID = carve(CSC + 20480, 18, 2048, BF16)
            WSTF = carve(20544, 128, 48, F32).rearrange("p (k c) -> p k c", k=8)
            WFF6 = carve(20736, 128, 48, BF16).rearrange("p (k c) -> p k c", k=8)
            WFF3 = carve(20832, 128, 144, BF16).rearrange("p (k c) -> p k c", k=8)
            BF3 = carve(21120, 18, 1, F32)
            NB3 = carve(21124, 18, 1, F32)
            RD = carve(21128, 128, 4, F32)

            DMA(WSTF, w_in3[:, :, O_FF:O_FF + 6], w=["WSTF"])
            DMA(BF3, bf3_d[l], w=["BF3"])
            OP("dve", lambda e: e.tensor_scalar(out=NB3, in0=BF3, scalar1=-1.0, scalar2=None, op0=ALU.mult),
               r=["BF3"], w=["NB3"])
            for k in range(8):
                OP("pool", lambda e, k=k: e.tensor_scalar(out=WFF6[:, k, :], in0=WSTF[:, k, :], scalar1=G[:, k:k + 1],
                                                          scalar2=0.0, op0=ALU.mult, op1=ALU.add),
                   r=["WSTF", "G"], w=[("WFF6", k)])
            for j in range(3):
                OP("pool", lambda e, j=j: e.tensor_copy(out=WFF3[:, :, j:18:3], in_=WFF6[:, :, :]),
                   r=[("WFF6", k) for k in range(8)], w=[("WFF3", j)])
            def cprep1():
                for tg in range(4):
                    pb, pk = FOXPB[tg % 3]

                    def f(e, tg=tg, pb=pb):
                        ins = None
                        for k in range(8):
                            ins = e.matmul(pb[0:18, 0:512], lhsT=WFF3[:, k, :], rhs=HT[:, k, tg * 512:(tg + 1) * 512],
                                           start=(k == 0), stop=(k == 7))
                        return ins
                    OP("pe", f, r=[("WFF3", j) for j in range(3)] + HTALL, w=[pk])
                    OP("act", lambda e, tg=tg, pb=pb: e.activation(out=EF[:, tg * 512:(tg + 1) * 512], in_=pb[0:18, 0:512],
                                                                    func=AF.Exp, bias=NB3, scale=-1.0),
                       r=[pk, "NB3"], w=[("EF", tg)])
                OP("act", lambda e: e.activation(out=EF, in_=EF, func=AF.Ln, bias=1.0, scale=1.0),
                   r=[("EF", tg) for tg in range(4)], w=["SPF"])

            def cprep2():
                S.mark()
                OP("dve", lambda e: e.tensor_tensor_scan(out=C32[:, 0:1024], data0=ONESB[0:18, 0:1024], data1=EF[:, 0:1024],
                                                         initial=0.0, op0=ALU.mult, op1=ALU.subtract),
                   r=["SPF", "ONESB"], w=["C32a"])
                S.mark()
                OP("dve", lambda e: e.tensor_tensor_scan(out=C32[:, 1024:2048], data0=ONESB[0:18, 0:1024], data1=EF[:, 1024:2048],
                                                         initial=C32[:, 1023:1024], op0=ALU.mult, op1=ALU.subtract),
                   r=["SPF", "ONESB", "C32a"], w=["C32b"])
                S.mark()
                OP("dve", lambda e: e.tensor_copy(out=HI, in_=C32), r=["C32a", "C32b"], w=["HI"])
                S.mark()
                OP("dve", lambda e: e.tensor_tensor(out=EF, in0=C32, in1=HI, op=ALU.subtract), r=["C32a", "C32b", "HI", "SPF"], w=["R1", "SPF"])
                S.mark()
                OP("dve", lambda e: e.tensor_copy(out=MID, in_=EF), r=["R1"], w=["MID"])
                S.mark()
                OP("dve", lambda e: e.tensor_tensor(out=C32, in0=EF, in1=MID, op=ALU.subtract), r=["R1", "MID", "HI"], w=["R2", "C32a", "C32b"])
                S.mark()
                OP("dve", lambda e: e.tensor_scalar(out=CT, in0=HI, scalar1=M3[:, 0:1], scalar2=None, op0=ALU.mult),
                   r=["HI", "m3"], w=["CT0"])
                S.mark()
                OP("dve", lambda e: e.scalar_tensor_tensor(out=CT, in0=MID, scalar=M3[:, 1:2], in1=CT, op0=ALU.mult, op1=ALU.add),
                   r=["MID", "CT0"], w=["CT1"])
                S.mark()
                OP("dve", lambda e: e.scalar_tensor_tensor(out=CT, in0=C32, scalar=M3[:, 2:3], in1=CT, op0=ALU.mult, op1=ALU.add),
                   r=["R2", "CT1"], w=["CT"])


            OP("pool", lambda e: e.memset(FK[64:70, :], 1.0), w=["FKc"])
            OP("pool", lambda e: e.memset(FQ[64:70, :], -1.0), w=["FQc"])
            OP("pool", lambda e: e.memset(VA, 1.0), w=["VA1"])

            FOXB = [OB[0]]
            SBBK = [OB[1]]
            FOXPB = [OB[0], STB[0], STB[1]]
            SBPB = [OB[1], STB[2], STB[3]]
            fox_st_rr = [0]

            def fox_st():
                fox_st_rr[0] ^= 1
                return STB[fox_st_rr[0]]
            S.capture_begin()
            for p in range(3):
                if p == 0 and "fox" in pref:
                    wb, wk = pref.pop("fox")
                else:
                    wb, wk = wload(w_in3, fox_specs(p), buf=0)

                def ev_tm(c, pb, pk):
                    OP("dve", lambda e, c=c, pb=pb: e.tensor_copy(out=VA[:, c, :, 0:64],
                                                                   in_=pb[:, 0:128].rearrange("p (h d) -> p h d", h=2)),
                       r=[pk, "VA1"], w=[("VA", c)])
                    OP("act", lambda e, c=c, pb=pb: e.activation(out=ZF[:, c, :], in_=pb[:, 128:256], func=AF.Silu),
                       r=[pk], w=[("ZF", c)])
                proj_tm(wb, wk, 256, 256, ev_tm, banks=FOXPB)
                if p == 0:
                    S.mark()
                    cprep1()
                for hh in range(2):
                    h = 2 * p + hh

                    def ev_q(tg, pb, pk):
                        OP("act", lambda e, tg=tg, pb=pb: e.activation(out=FQ[0:64, tg * 512:(tg + 1) * 512],
                                                                        in_=pb[0:64, 0:512], func=AF.Copy, scale=0.125),
                           r=[pk], w=[("FQ", tg)])

                    def ev_k(tg, pb, pk):
                        OP("dve", lambda e, tg=tg, pb=pb: e.tensor_copy(out=FK[0:64, tg * 512:(tg + 1) * 512], in_=pb[0:64, 0:512]),
                           r=[pk], w=[("FK", tg)])
                    proj_fm(wb, wk, hh * 64, 64, ev_q, banks=FOXPB)
                    proj_fm(wb, wk, 128 + hh * 64, 64, ev_k, banks=FOXPB)
                    if h == 0:
                        S.mark()
                        cprep2()
                        S.done("CT")
                    DMA(FK[64:67, :], CT[3 * h:3 * h + 3, :], r=["CT", "FKc"], w=["FKa"])
                    DMA(FQ[67:70, :], CT[3 * h:3 * h + 3, :], r=["CT", "FQc"], w=["FQa"])
                    FQK = [("FQ", tg) for tg in range(4)] + [("FK", tg) for tg in range(4)] + ["FKa", "FQa", "FKc", "FQc"]

                    tiles = []
                    for g in range(4):
                        for cs in range(4 * g + 4):
                            tiles.append((g, cs))
                    state = {}

                    def stA(n):
                        g, cs = tiles[n]
                        r = max(0, cs - 4 * g)
                        c0 = 128 * r
                        st, stk = fox_st()
                        ptl, ptk = next_ptl()
                        state[n] = (st, stk, ptl, ptk, r, c0)

                        def f(e, g=g, cs=cs, st=st, c0=c0, r=r):
                            ins = e.matmul(st[:, c0:512], lhsT=FK[0:70, cs * 128:(cs + 1) * 128],
                                           rhs=FQ[0:70, g * 512 + c0:(g + 1) * 512], start=True, stop=(cs < 4 * g))
                            if cs >= 4 * g:
                                ins = e.matmul(st[:, c0:c0 + 128], lhsT=IDENT[:], rhs=MDLE4[:, 0:128], start=False, stop=True)
                            return ins
                        OP("pe", f, r=FQK + ["ident", "mdle4"], w=[stk])
                        OP("act", lambda e, st=st, ptl=ptl, c0=c0: e.activation(out=ptl[:, c0:512], in_=st[:, c0:512], func=AF.Exp),
                           r=[stk], w=[ptk])

                    def stC(n):
                        g, cs = tiles[n]
                        st, stk, ptl, ptk, r, c0 = state.pop(n)
                        ob, obk = FOXB[0]

                        def f(e, g=g, cs=cs, ptl=ptl, r=r, ob=ob, hh=hh):
                            ins = None
                            for tb in range(r, 4):
                                ins = e.matmul(ob[:, tb * 65:(tb + 1) * 65], lhsT=ptl[:, tb * 128:(tb + 1) * 128],
                                               rhs=VA[:, cs, hh, :], start=(cs == 0 and tb == 0), stop=(cs == 4 * g + tb),
                                               skip_group_check=True)
                            return ins
                        OP("pe", f, r=[ptk, ("VA", cs)], w=[obk])
                        if cs == 4 * g + 3:
                            OP("dve", lambda e, ob=ob: e.reciprocal(
                                out=RD[:, 0:4], in_=ob[:, 0:260].rearrange("p (t d) -> p t d", d=65)[:, :, 64]),
                               r=[obk], w=["RD"])
                            for tb in range(4):
                                c = 4 * g + tb
                                OP("dve", lambda e, ob=ob, tb=tb, c=c, h=h, hh=hh: e.scalar_tensor_tensor(
                                    out=M[:, c, h * 64:(h + 1) * 64], in0=ob[:, tb * 65:tb * 65 + 64], scalar=RD[:, tb:tb + 1],
                                    in1=ZF[:, c, hh * 64:(hh + 1) * 64], op0=ALU.mult, op1=ALU.mult),
                                   r=[obk, "RD", ("ZF", c)], w=[("M", c, h)])

                    DEPTH = 2
                    for m in range(len(tiles) + DEPTH):
                        S.mark()
                        if m - DEPTH >= 0:
                            stC(m - DEPTH)
                        if m < len(tiles):
                            stA(m)
            fox_ops = S.capture_end()

            S.capture_begin()
            SQt = carve(SBB + 0, 128, 2048, BF16)
            SKR = carve(SBB + 4096, 128, 2048, BF16)
            VR = carve(SBB + 8192, 128, NCH * 128, BF16).rearrange("p (c d) -> p c d", c=NCH)
            ZS = carve(SBB + 12288, 128, NCH * 128, BF16).rearrange("p (c d) -> p c d", c=NCH)
            KTM = carve(SBB + 16384, 128, NCH * 128, BF16).rearrange("p (c d) -> p c d", c=NCH)
            VTM = carve(SBB + 20480, 128, NCH * 128, BF16).rearrange("p (c d) -> p c d", c=NCH)
            EB = [carve(SBB + 24576 + 2048 * j, 128, 512, F32) for j in range(4)]
            LPB = [carve(SBB + 32768 + 2048 * j, 128, 512, F32) for j in range(2)]
            GX = [carve(SBB + 36864 + 2056 * j, 128, 514, F32) for j in range(3)]
            AB = [carve(SBB + 47104 + 1024 * j, 128, 512, BF16) for j in range(2)]
            ATB = [carve(SBB + 49152 + 1024 * j, 128, 512, BF16) for j in range(2)]
            sb_st_rr = [0]

            def sb_st():
                sb_st_rr[0] ^= 1
                return STB[2 + sb_st_rr[0]]

            for p in range(3):
                if p == 0 and "sb" in pref:
                    wb, wk = pref.pop("sb")
                else:
                    wb, wk = wload(w_in3, sb_specs(p), buf=1)

                def ev_tm(c, pb, pk):
                    OP("dve", lambda e, c=c, pb=pb: e.tensor_copy(out=KTM[:, c, :], in_=pb[:, 0:128]), r=[pk], w=[("KTM", c)])
                    OP("dve", lambda e, c=c, pb=pb: e.tensor_copy(out=VTM[:, c, :], in_=pb[:, 128:256]), r=[pk], w=[("VTM", c)])
                    OP("act", lambda e, c=c, pb=pb: e.activation(out=ZS[:, c, :], in_=pb[:, 256:384], func=AF.Silu),
                       r=[pk], w=[("ZS", c)])
                proj_tm(wb, wk, 128, 384, ev_tm, banks=SBPB)

                def ev_q(tg, pb, pk):
                    OP("act", lambda e, tg=tg, pb=pb: e.activation(out=SQt[:, tg * 512:(tg + 1) * 512], in_=pb[:, 0:512],
                                                                    func=AF.Copy, scale=0.125), r=[pk], w=[("SQ", tg)])
                proj_fm(wb, wk, 0, 128, ev_q, banks=SBPB)
                for c0 in range(0, NCH, 4):
                    S.mark()
                    pbk, pkk = SBBK[0]
                    pbv, pkv = SBBK[0]

                    def fk(e, c0=c0, pbk=pbk):
                        ins = None
                        for c in range(c0, c0 + 4):
                            sl = 3 - (c - c0)
                            ins = e.matmul(pbk[:, sl * 128:(sl + 1) * 128], lhsT=KTM[:, c, :], rhs=JM[:], start=True, stop=True)
                        return ins
                    OP("pe", fk, r=[("KTM", c) for c in range(c0, c0 + 4)] + ["jm"], w=[pkk])
                    q0 = (15 - c0 - 3) * 128
                    OP("act", lambda e, pbk=pbk, q0=q0: e.activation(out=SKR[:, q0:q0 + 512], in_=pbk[:, 0:512], func=AF.Copy),
                       r=[pkk], w=[("SKR", (15 - c0 - 3) // 4)])

                    def fv(e, c0=c0, pbv=pbv):
                        ins = None
                        for c in range(c0, c0 + 4):
                            sl = 3 - (c - c0)
                            ins = e.matmul(pbv[:, sl * 128:(sl + 1) * 128], lhsT=JM[:], rhs=VTM[:, c, :], start=True, stop=True)
                        return ins
                    OP("pe", fv, r=[("VTM", c) for c in range(c0, c0 + 4)] + ["jm"], w=[pkv])
                    OP("dve", lambda e, pbv=pbv, c0=c0: e.tensor_copy(
                        out=VR[:, 15 - c0 - 3:16 - c0, :], in_=pbv[:, 0:512].rearrange("p (c d) -> p c d", c=4)),
                       r=[pkv], w=[("VR", (15 - c0 - 3) // 4)])
                SQK = [("SQ", tg) for tg in range(4)] + [("SKR", tg) for tg in range(4)]

                tiles = []
                for hh in range(2):
                    for i in range(NCH):
                        k0 = 128 * (15 - i)
                        j = 0
                        while k0 < 2048:
                            k1 = min(k0 + 512, 2048)
                            tiles.append((hh, i, j, k0, k1))
                            k0 = k1
                            j += 1
                NT = len(tiles)
                stn = {}

                def s0(n):
                    hh, i, j, k0, k1 = tiles[n]
                    st, stk = sb_st()
                    stn[n] = (st, stk)
                    nn = k1 - k0

                    def f(e, hh=hh, i=i, j=j, k0=k0, k1=k1, st=st, nn=nn):
                        ins = e.matmul(st[:, 0:nn], lhsT=SQt[hh * 64:(hh + 1) * 64, i * 128:(i + 1) * 128],
                                       rhs=SKR[hh * 64:(hh + 1) * 64, k0:k1], start=True, stop=(j != 0))
                        if j == 0:
                            ins = e.matmul(st[:, 0:128], lhsT=IDENT[:], rhs=MSBR[:], start=False, stop=True)
                        return ins
                    OP("pe", f, r=SQK + ["ident", "msbr"], w=[stk])

                def s1(n):
                    hh, i, j, k0, k1 = tiles[n]
                    st, stk = stn.pop(n)
                    nn = k1 - k0
                    eb = EB[n % 4]
                    OP("act", lambda e, st=st, eb=eb, nn=nn: e.activation(out=eb[:, 0:nn], in_=st[:, 0:nn], func=AF.Tanh, scale=-0.5),
                       r=[stk, "CT"], w=[("EB", n % 4)])
                    OP("act", lambda e, eb=eb, nn=nn: e.activation(out=eb[:, 0:nn], in_=eb[:, 0:nn], func=AF.Copy, bias=0.5, scale=0.5),
                       r=[("EB", n % 4), "CT"], w=[("EB", n % 4)])

                def s2(n):
                    hh, i, j, k0, k1 = tiles[n]
                    nn = k1 - k0
                    eb, gx = EB[n % 4], GX[n % 3]
                    if j == 0:
                        init, rk = 1.0, []
                        OP("act", lambda e, gx=gx: e.activation(out=gx[:, 0:1], in_=ONESB[:, 0:1], func=AF.Copy), r=["CT", "ONESB"], w=[("GX0", n % 3)])
                    else:
                        pn = tiles[n - 1][4] - tiles[n - 1][3]
                        pgx = GX[(n - 1) % 3]
                        init, rk = pgx[:, pn:pn + 1], [("GX", (n - 1) % 3)]
                        OP("act", lambda e, gx=gx, pgx=pgx, pn=pn: e.activation(out=gx[:, 0:1], in_=pgx[:, pn:pn + 1], func=AF.Copy),
                           r=rk + ["CT"], w=[("GX0", n % 3)])
                    OP("dve", lambda e, eb=eb, gx=gx, nn=nn, init=init: e.tensor_tensor_scan(
                        out=gx[:, 1:nn + 1], data0=eb[:, 0:nn], data1=ONESB[:, 0:nn], initial=init, op0=ALU.mult, op1=ALU.mult),
                       r=[("EB", n % 4), "ONESB", "CT"] + rk, w=[("GX", n % 3)])

                def s3(n):
                    pass

                def s4(n):
                    hh, i, j, k0, k1 = tiles[n]
                    nn = k1 - k0
                    gx, ab = GX[n % 3], AB[n % 2]
                    OP("pool", lambda e, gx=gx, ab=ab, nn=nn: e.tensor_tensor(out=ab[:, 0:nn], in0=gx[:, 0:nn], in1=gx[:, 1:nn + 1], op=ALU.subtract),
                       r=[("GX", n % 3), ("GX0", n % 3), "CT"], w=[("AB", n % 2)])

                def s5(n):
                    hh, i, j, k0, k1 = tiles[n]
                    nn = k1 - k0
                    ab = AB[n % 2]
                    pt, ptk = PTB[n % 2]

                    def f(e, ab=ab, pt=pt, nn=nn):
                        ins = None
                        for blk in range(nn // 128):
                            ins = e.transpose(out=pt[:, blk * 128:(blk + 1) * 128], in_=ab[:, blk * 128:(blk + 1) * 128], identity=IDENT[:])
                        return ins
                    OP("pe", f, r=[("AB", n % 2), "ident"], w=[ptk])

                def s6(n):
                    hh, i, j, k0, k1 = tiles[n]
                    nn = k1 - k0
                    pt, ptk = PTB[n % 2]
                    atb = ATB[n % 2]
                    OP("dve", lambda e, pt=pt, atb=atb, nn=nn: e.tensor_copy(out=atb[:, 0:nn], in_=pt[:, 0:nn]),
                       r=[ptk, "CT"], w=[("ATB", n % 2)])

                def s7(n):
                    hh, i, j, k0, k1 = tiles[n]
                    nn = k1 - k0
                    atb = ATB[n % 2]
                    ob, obk = SBBK[0]
                    h = 2 * p + hh
                    last = (k1 == 2048)

                    def f(e, atb=atb, ob=ob, hh=hh, j=j, k0=k0, nn=nn, last=last):
                        ins = None
                        nblk = nn // 128
                        for blk in range(nblk):
                            ins = e.matmul(ob[:, 0:64], lhsT=atb[:, blk * 128:(blk + 1) * 128],
                                           rhs=VR[:, k0 // 128 + blk, hh * 64:(hh + 1) * 64],
                                           start=(j == 0 and blk == 0), stop=(last and blk == nblk - 1))
                        return ins
                    OP("pe", f, r=[("ATB", n % 2)] + [("VR", q) for q in range(4)], w=[obk])
                    if last:
                        OP("dve", lambda e, ob=ob, i=i, h=h, hh=hh: e.tensor_tensor(
                            out=M[:, i, 384 + h * 64:384 + (h + 1) * 64], in0=ob[:, 0:64], in1=ZS[:, i, hh * 64:(hh + 1) * 64],
                            op=ALU.mult), r=[obk, ("ZS", i)], w=[("M", i, 6 + h)])

                stages = [s0, s1, s2, s3, s4, s5, s6, s7]
                for k in range(NT + len(stages)):
                    S.mark()
                    if p == 0 and k == 0:
                        S.need("CT")
                    for sidx in reversed(range(len(stages))):
                        n = k - sidx
                        if 0 <= n < NT:
                            stages[sidx](n)
            sb_ops = S.capture_end()
            merged = Sched.merge(fox_ops, sb_ops)
            ct_idx = max(i for i, o in enumerate(merged) if "CT" in o.get("writes", ()))
            eb_idx = min(i for i, o in enumerate(merged) if any(isinstance(w, tuple) and w[0] in ("EB", "GX", "GX0", "AB", "ATB")
                                                                for w in o.get("writes", ())))
            assert ct_idx < eb_idx, (ct_idx, eb_idx)
            S.ops.extend(merged)
            S.barrier()

            if "nsa" not in phases:
                return
            NQ = carve(0, 128, 4 * 2048, BF16)
            NQ3 = NQ.rearrange("p (g t) -> p g t", g=4)
            NKS = carve(16384, 128, 2048, BF16)
            NKW = carve(20480, 128, 2048, BF16)
            KC = carve(24576, 128, 128, BF16)
            KCT = carve(24832, 64, 2048, BF16)
            VCT = carve(28928, 64, 2048, BF16)
            VS = carve(33024, 128, NCH * 65, BF16).rearrange("p (c d) -> p c d", c=NCH)
            VW = carve(35104, 128, NCH * 65, BF16).rearrange("p (c d) -> p c d", c=NCH)
            ZN = carve(37184, 128, NCH * 256, BF16).rearrange("p (c d) -> p c d", c=NCH)
            GT = carve(45376, 128, NCH * 12, F32).rearrange("p (c d) -> p c d", c=NCH)
            W1S = carve(46144, 64, 2048, F32)
            W1B = [carve(54336, 64, 2048, BF16).rearrange("p (l e) -> p l e", l=32),
                   carve(58432, 64, 2048, BF16).rearrange("p (l e) -> p l e", l=32)]
            HID = [carve(62528, 64, 128, BF16), carve(62784, 64, 128, BF16)]
            W2S = carve(63040, 64, 64, F32)
            W2P = [carve(63296, 64, 128, BF16), carve(63552, 64, 128, BF16)]
            POS = carve(63808, 64, 32, F32)
            POSB = carve(63936, 64, 32, BF16)
            PBIAS = [carve(64000, 64, 1, F32), carve(64004, 64, 1, F32)]
            IMP = carve(64008, 128, 32, F32)
            MX8 = carve(64136, 128, 8, F32)
            SELM = carve(64168, 128, 32, F32)
            NEGB = carve(64296, 128, 32, BF16)
            DEN = carve(64360, 128, 12, F32)
            COEF = carve(64408, 128, 12, F32)
            ACC = carve(64456, 128, 64, F32)

            DMA(NQ[0:64, :], tab_d["nq_init"], w=["NQi"])
            DMA(NKS[0:64, :], tab_d["nks_init"], w=["NKSi"])
            DMA(NKW[0:64, :], tab_d["nkw_init"], w=["NKWi"])
            DMA(KC[0:64, :], tab_d["kc_init"], w=["KCi"])
            OP("pool", lambda e: e.memset(VS, 1.0), w=["VS1"])
            OP("pool", lambda e: e.memset(VW, 1.0), w=["VW1"])

            wb, wk = wload(w_in3, [(O_NKC, 64, 0), (O_NQ, 256, 64), (O_NVC, 64, 320), (O_NKS, 64, 384), (O_NKW, 64, 448)])
            for g in range(4):
                def ev(tg, pb, pk, g=g):
                    OP("act", lambda e, tg=tg, pb=pb, g=g: e.activation(out=NQ3[64:128, g, tg * 512:(tg + 1) * 512],
                                                                         in_=pb[64:128, 0:512], func=AF.Copy, scale=0.125),
                       r=[pk], w=[("NQ", g, tg)])
                proj_fm(wb, wk, 64 * g, 128, ev)

            def ev_ks(tg, pb, pk):
                OP("dve", lambda e, tg=tg, pb=pb: e.tensor_copy(out=NKS[64:128, tg * 512:(tg + 1) * 512], in_=pb[64:128, 0:512]),
                   r=[pk], w=[("NKS", tg)])

            def ev_kw(tg, pb, pk):
                OP("dve", lambda e, tg=tg, pb=pb: e.tensor_copy(out=NKW[64:128, tg * 512:(tg + 1) * 512], in_=pb[64:128, 0:512]),
                   r=[pk], w=[("NKW", tg)])

            def ev_kc(tg, pb, pk):
                OP("dve", lambda e, tg=tg, pb=pb: e.tensor_copy(out=KCT[:, tg * 512:(tg + 1) * 512], in_=pb[0:64, 0:512]),
                   r=[pk], w=[("KCT", tg)])

            def ev_vc(tg, pb, pk):
                OP("act", lambda e, tg=tg, pb=pb: e.activation(out=VCT[:, tg * 512:(tg + 1) * 512], in_=pb[0:64, 0:512], func=AF.Copy),
                   r=[pk], w=[("VCT", tg)])
            proj_fm(wb, wk, 320, 128, ev_ks)
            proj_fm(wb, wk, 384, 128, ev_kw)
            proj_fm(wb, wk, 0, 64, ev_kc)
            proj_fm(wb, wk, 320, 64, ev_vc)

            wb, wk = wload(w_in3, [(O_NVS, 64, 0), (O_NVW, 64, 64), (O_NG, 12, 128), (O_NZ, 256, 140)])

            def ev_n2(c, pb, pk):
                OP("dve", lambda e, c=c, pb=pb: e.tensor_copy(out=VS[:, c, 0:64], in_=pb[:, 0:64]), r=[pk, "VS1"], w=[("VS", c)])
                OP("dve", lambda e, c=c, pb=pb: e.tensor_copy(out=VW[:, c, 0:64], in_=pb[:, 64:128]), r=[pk, "VW1"], w=[("VW", c)])
                OP("dve", lambda e, c=c, pb=pb: e.tensor_copy(out=GT[:, c, :], in_=pb[:, 128:140]), r=[pk], w=[("GTraw", c)])
                OP("act", lambda e, c=c, pb=pb: e.activation(out=ZN[:, c, :], in_=pb[:, 140:396], func=AF.Silu),
                   r=[pk], w=[("ZN", c)])
            proj_tm(wb, wk, 0, 396, ev_n2)
            OP("act", lambda e: e.activation(out=GT, in_=GT, func=AF.Sigmoid), r=[("GTraw", c) for c in range(NCH)],
               w=[("GT", c) for c in range(NCH)])

            for kv, (w1_d, w2_d, pos_d, SRC, srck) in enumerate([(w1k_d, w2k_d, posk_d, KCT, "KCT"), (w1v_d, w2v_d, posv_d, VCT, "VCT")]):
                DMA(W1S, w1_d[l], w=["W1S"])
                DMA(W2S, w2_d[l], w=["W2S"])
                DMA(POS, pos_d[l], w=["POS"])
                OP("pool", lambda e, kv=kv: e.tensor_copy(out=W1B[kv].rearrange("p l e -> p (l e)"), in_=W1S), r=["W1S"], w=[("W1B", kv)])
                OP("pool", lambda e, kv=kv: e.tensor_copy(out=W2P[kv][:, 64:128], in_=W2S), r=["W2S"], w=[("W2P", kv)])
                OP("pool", lambda e: e.tensor_copy(out=POSB, in_=POS), r=["POS"], w=["POSB"])
                pb, pk = OB[kv]

                def fpb(e, kv=kv, pb=pb):
                    ins = None
                    for ll in range(32):
                        ins = e.matmul(pb[0:64, 0:1], lhsT=W1B[kv][:, ll, :], rhs=POSB[:, ll:ll + 1], start=(ll == 0), stop=(ll == 31))
                    return ins
                OP("pe", fpb, r=[("W1B", kv), "POSB"], w=[pk])
                OP("dve", lambda e, kv=kv, pb=pb: e.tensor_copy(out=PBIAS[kv], in_=pb[0:64, 0:1]), r=[pk], w=[("PBIAS", kv)])
                pb2, pk2 = OB[1 - kv]

                def fh(e, kv=kv, pb2=pb2, SRC=SRC):
                    ins = None
                    for ll in range(32):
                        ins = e.matmul(pb2[0:64, 0:127], lhsT=W1B[kv][:, ll, :], rhs=SRC[:, ll:ll + 16 * 126 + 1:16],
                                       start=(ll == 0), stop=(ll == 31))
                    return ins
                OP("pe", fh, r=[("W1B", kv)] + [(srck, tg) for tg in range(4)], w=[pk2])
                OP("act", lambda e, kv=kv, pb2=pb2: e.activation(out=HID[kv][:, 0:127], in_=pb2[0:64, 0:127], func=AF.Silu,
                                                                   bias=PBIAS[kv], scale=1.0),
                   r=[pk2, ("PBIAS", kv)], w=[("HID", kv)])
                if kv == 0:
                    OP("pool", lambda e: e.memset(W2P[0][:, 0:64], 0.0), w=["W2Pz"])
                    OP("pe", lambda e, pb=pb: e.matmul(pb[:, 0:127], lhsT=W2P[0][:, :], rhs=HID[0][:, 0:127], start=True, stop=True),
                       r=[("W2P", 0), "W2Pz", ("HID", 0), ("PBIAS", 0)], w=[pk])
                    OP("dve", lambda e, pb=pb: e.tensor_copy(out=KC[64:128, 0:127], in_=pb[64:128, 0:127]), r=[pk, "KCi"], w=["KC"])
                else:
                    OP("pe", lambda e, pb=pb: e.matmul(pb[0:127, 0:64], lhsT=HID[1][:, 0:127], rhs=W2P[1][:, 64:128], start=True, stop=True),
                       r=[("W2P", 1), ("HID", 1), ("PBIAS", 1)], w=[pk])
                    OP("dve", lambda e, pb=pb: e.tensor_copy(out=VC[0:127, 0:64], in_=pb[0:127, 0:64]), r=[pk, "VCc"], w=["VC"])

            wbs = []
            for half in range(2):
                wbs.append(wload(w_out3, [(512 * half, 512, 0)], scale_g=False))
            NQK = [("NQ", g, tg) for g in range(4) for tg in range(4)] + ["NQi"]
            tiles = [("cmp", 0, 0)]
            for i in range(NCH):
                if i + 1 < NCH:
                    tiles.append(("cmp", i + 1, 0))
                for cs in range(max(0, i - 4), i + 1):
                    tiles.append(("win", i, cs))
                for cs in range(i + 1):
                    tiles.append(("sel", i, cs))
            state = {}
            OCMP = (PB[:, 512:1024], "PB1")
            ST3 = [STB[0], STB[1], STB[2]]
            st3_rr = [0]

            def hook_sel(i):
                OP("pe", lambda e: e.transpose(out=PT1[0:32, 0:128], in_=NEGBs[i % 2], identity=IDENT[:]), r=[("NEGB", i % 2), "ident"], w=["PT1"])
                OP("dve", lambda e, i=i: e.tensor_copy(out=NQ3[0:32, :, i * 128:(i + 1) * 128],
                                                       in_=PT1[0:32, 0:128].unsqueeze(1).to_broadcast([32, 4, 128])),
                   r=["PT1"], w=[("NQsel", i)])

            def stA(n):
                kind, i, cs = tiles[n]
                if kind == "sel" and cs == 0:
                    hook_sel(i)
                st3_rr[0] = (st3_rr[0] + 1) % 3
                st, stk = ST3[st3_rr[0]]
                ptl, ptk = next_ptl()
                state[n] = (st, stk, ptl, ptk)
                rhs = NQ3[:, :, i * 128:(i + 1) * 128]
                if kind == "cmp":
                    ni = min(127, 8 * i + 7)

                    def f(e, st=st, rhs=rhs, i=i, ni=ni):
                        e.matmul(st[0:ni, 0:512], lhsT=KC[:, 0:ni], rhs=rhs, start=True, stop=False)
                        return e.matmul(st[0:ni, 0:512], lhsT=BIGBAND[:, 128 - 8 * i:128 - 8 * i + ni], rhs=MPAT[:, :],
                                        start=False, stop=True)
                    OP("pe", f, r=NQK + ["KC", "KCi", "bigband", "mpat", ("NQsel", i)], w=[stk])
                    OP("act", lambda e, st=st, ptl=ptl, ni=ni: e.activation(out=ptl[0:ni, :], in_=st[0:ni, 0:512], func=AF.Exp),
                       r=[stk], w=[ptk])
                else:
                    KT, kkeys = (NKW, [("NKW", tg) for tg in range(4)] + ["NKWi"]) if kind == "win" else \
                                (NKS, [("NKS", tg) for tg in range(4)] + ["NKSi"])
                    mask = None
                    if cs == i:
                        mask = MDLE4
                    elif kind == "win" and cs == i - 4:
                        mask = MDFAR4

                    def f(e, st=st, rhs=rhs, KT=KT, cs=cs, mask=mask):
                        ins = e.matmul(st[:, 0:512], lhsT=KT[:, cs * 128:(cs + 1) * 128], rhs=rhs, start=True, stop=(mask is None))
                        if mask is not None:
                            ins = e.matmul(st[:, 0:512], lhsT=IDENT[:], rhs=mask[:, :], start=False, stop=True)
                        return ins
                    OP("pe", f, r=NQK + kkeys + ["ident", "mdle4", "mdfar4", ("NQsel", i)], w=[stk])
                    OP("act", lambda e, st=st, ptl=ptl: e.activation(out=ptl[:, :], in_=st[:, 0:512], func=AF.Exp),
                       r=[stk], w=[ptk])

            def stC(n):
                kind, i, cs = tiles[n]
                st, stk, ptl, ptk = state.pop(n)
                if kind == "cmp":
                    ni = min(127, 8 * i + 7)
                    ob, obk = OCMP

                    def f(e, ptl=ptl, ni=ni, ob=ob):
                        ins = None
                        for g in range(4):
                            ins = e.matmul(ob[:, g * 97:(g + 1) * 97], lhsT=ptl[0:ni, g * 128:(g + 1) * 128], rhs=VC[0:ni, :],
                                           start=(g == 0), stop=True, skip_group_check=True)
                        return ins
                    OP("pe", f, r=[ptk, "VC", "VCc"], w=[obk])
                    ob3 = ob[:, 0:388].rearrange("p (g d) -> p g d", g=4)
                    OP("dve", lambda e, ob3=ob3: e.tensor_scalar(out=DENs[i % 2][:, 0:4], in0=ob3[:, :, 64], scalar1=1e-30, scalar2=None, op0=ALU.max),
                       r=[obk], w=[("DENc", i % 2)])
                    OP("dve", lambda e: e.reciprocal(out=DENs[i % 2][:, 0:4], in_=DENs[i % 2][:, 0:4]), r=[("DENc", i % 2)], w=[("RDc", i % 2)])
                    for g in range(4):
                        if g == 0:
                            OP("dve", lambda e, ob3=ob3: e.tensor_scalar(out=IMP, in0=ob3[:, 0, 65:97], scalar1=DENs[i % 2][:, 0:1], scalar2=None,
                                                                         op0=ALU.mult), r=[obk, ("RDc", i % 2)], w=["IMP"])
                        else:
                            OP("dve", lambda e, ob3=ob3, g=g: e.scalar_tensor_tensor(out=IMP, in0=ob3[:, g, 65:97], scalar=DENs[i % 2][:, g:g + 1],
                                                                                   in1=IMP, op0=ALU.mult, op1=ALU.add),
                               r=[obk, ("RDc", i % 2), "IMP"], w=["IMP"])
                    OP("dve", lambda e, i=i: e.tensor_tensor(out=IMP, in0=IMP, in1=ADDM[:, i, :], op=ALU.add), r=["IMP", "addm"], w=["IMP"])
                    OP("dve", lambda e: e.max(out=MX8, in_=IMP), r=["IMP"], w=["MX8"])
                    OP("dve", lambda e: e.tensor_scalar(out=SELM, in0=IMP, scalar1=MX8[:, 7:8], scalar2=None, op0=ALU.is_ge),
                       r=["IMP", "MX8"], w=["SELM"])
                    OP("dve", lambda e: e.tensor_scalar(out=NEGBs[i % 2], in0=SELM, scalar1=-NEG, scalar2=NEG, op0=ALU.mult, op1=ALU.add),
                       r=["SELM"], w=[("NEGB", i % 2)])
                    OP("dve", lambda e, i=i: e.tensor_tensor(out=COEFs[i % 2][:, 0:4], in0=DENs[i % 2][:, 0:4], in1=GT[:, i, 0:12:3], op=ALU.mult),
                       r=[("RDc", i % 2), ("GT", i)], w=[("COEFc", i % 2)])
                    for g in range(4):
                        OP("dve", lambda e, ob3=ob3, g=g, i=i: e.tensor_scalar(
                            out=OACCs[i % 2][:, g, :], in0=ob3[:, g, 0:64], scalar1=COEFs[i % 2][:, g:g + 1], scalar2=None, op0=ALU.mult),
                           r=[obk, ("COEFc", i % 2), ("OACC", g, i % 2)], w=[("OACC", g, i % 2)])
                    return
                VT, vk = (VW, "VW") if kind == "win" else (VS, "VS")
                ob, obk = OB[0] if kind == "sel" else OB[1]
                first = (cs == (0 if kind == "sel" else max(0, i - 4)))
                lastt = (cs == i)

                def f(e, ptl=ptl, ob=ob, VT=VT, cs=cs, first=first, lastt=lastt):
                    ins = None
                    for g in range(4):
                        ins = e.matmul(ob[:, g * 65:(g + 1) * 65], lhsT=ptl[:, g * 128:(g + 1) * 128], rhs=VT[:, cs, :],
                                       start=(first and g == 0), stop=lastt, skip_group_check=True)
                    return ins
                OP("pe", f, r=[ptk, (vk, cs)], w=[obk])
                if lastt:
                    ob3 = ob[:, 0:260].rearrange("p (g d) -> p g d", g=4)
                    bi = 1 if kind == "sel" else 2
                    dk = ("DEN", bi, i % 2)
                    OP("dve", lambda e, ob3=ob3, bi=bi: e.reciprocal(out=DENs[i % 2][:, 4 * bi:4 * bi + 4], in_=ob3[:, :, 64]), r=[obk], w=[dk])
                    OP("dve", lambda e, bi=bi, i=i: e.tensor_tensor(out=COEFs[i % 2][:, 4 * bi:4 * bi + 4], in0=DENs[i % 2][:, 4 * bi:4 * bi + 4],
                                                                    in1=GT[:, i, bi:12:3], op=ALU.mult),
                       r=[dk, ("GT", i)], w=[("COEF", bi, i % 2)])
                    for g in range(4):
                        OP("dve", lambda e, ob3=ob3, g=g, bi=bi: e.scalar_tensor_tensor(
                            out=OACCs[i % 2][:, g, :], in0=ob3[:, g, 0:64], scalar=COEFs[i % 2][:, 4 * bi + g:4 * bi + g + 1], in1=OACCs[i % 2][:, g, :],
                            op0=ALU.mult, op1=ALU.add), r=[obk, ("COEF", bi, i % 2), ("OACC", g, i % 2)], w=[("OACC", g, i % 2)])
                    if kind == "sel":
                        OP("dve", lambda e, i=i: e.tensor_tensor(
                            out=M[:, i, 768:1024].rearrange("p (g d) -> p g d", g=4), in0=OACCs[i % 2][:, :, :],
                            in1=ZN[:, i, :].rearrange("p (g d) -> p g d", g=4), op=ALU.mult),
                           r=[("OACC", g, i % 2) for g in range(4)] + [("ZN", i)], w=[("M", i, 12)])

            OACCs = [SMALL[:, 0:256].rearrange("p (g d) -> p g d", g=4),
                     carve(64712, 128, 256, F32).rearrange("p (g d) -> p g d", g=4)]
            NEGBs = [NEGB, carve(65736, 128, 32, BF16)]
            DENs = [DEN, carve(65800, 128, 12, F32)]
            COEFs = [COEF, carve(65848, 128, 12, F32)]
            DEPTH = 3
            next_c = [0]
            cmp_idx = {}
            for n_, (kind_, i_, cs_) in enumerate(tiles):
                if kind_ == "cmp":
                    cmp_idx[i_] = n_
            for m in range(len(tiles) + DEPTH):
                while next_c[0] <= m - DEPTH:
                    stC(next_c[0])
                    next_c[0] += 1
                if m < len(tiles):
                    kind_, i_, cs_ = tiles[m]
                    if kind_ == "sel" and cs_ == 0:
                        while next_c[0] <= cmp_idx[i_]:
                            stC(next_c[0])
                            next_c[0] += 1
                    stA(m)
            S.barrier()

            if "out" not in phases:
                return
            for c in range(NCH):
                transpose_into_HT(M[:, c, :], [("M", c, j) for j in range(13)], c)
            evs = {}

            def o_dma(n):
                b = n % 2
                DMA(XT[b][:], xsrc(n), r=[("XS", n)], w=["XT%d" % b])
                if nxt == "seq":
                    DMA(XT[2 + b][:], x_d[s + 1][n * 128:(n + 1) * 128, :], w=["XT%d" % (2 + b)])

            def o_mm(n):
                pacc, pk0, pk1 = (PA, "PA0", "PA1") if n % 2 == 0 else (PB, "PB0", "PB1")

                def f(e, c=n, pacc=pacc):
                    ins = None
                    for half in range(2):
                        wb = wbs[half][0]
                        for k in range(8):
                            ins = e.matmul(pacc[:, half * 512:(half + 1) * 512], lhsT=HT[:, k, c * 128:(c + 1) * 128],
                                           rhs=wb[:, k, 0:512], start=(k == 0), stop=(k == 7))
                    return ins
                OP("pe", f, r=[("HT", n)] + wbs[0][1] + wbs[1][1], w=[pk0, pk1])

            def o_chain(n):
                b = n % 2
                c = n
                pacc, pk0, pk1 = (PA, "PA0", "PA1") if n % 2 == 0 else (PB, "PB0", "PB1")
                OP("dve", lambda e, b=b, pacc=pacc: e.tensor_tensor(out=XT[b][:], in0=XT[b][:], in1=pacc[:, :], op=ALU.add),
                   r=["XT%d" % b, pk0, pk1], w=["XT%d" % b])
                if not last:
                    DMA(xs_d[c * 128:(c + 1) * 128, :], XT[b][:], r=["XT%d" % b], w=[("XS", c)])
                    if nxt == "layer":
                        xn, xnk = stats_scale(XT[b], "XT%d" % b, c, n)
                        evs[n] = tpose(xn, [xnk], n)
                else:
                    OP("dve", lambda e, b=b, c=c: e.scalar_tensor_tensor(
                        out=XN[2][:], in0=XT[b][:], scalar=1.0, in1=XT[b][:], op0=ALU.mult, op1=ALU.mult,
                        accum_out=SS[:, c:c + 1]), r=["XT%d" % b], w=["XJ", ("SS", c)])
                    OP("act", lambda e, c=c: e.activation(out=SQ_[:, c:c + 1], in_=SS[:, c:c + 1], func=AF.Sqrt,
                                                          bias=EPS, scale=1.0 / D), r=[("SS", c)], w=[("SQ", c)])
                    OP("dve", lambda e, c=c: e.reciprocal(out=RS[:, c:c + 1], in_=SQ_[:, c:c + 1]), r=[("SQ", c)], w=[("RS", c)])
                    OP("dve", lambda e, b=b, c=c: e.scalar_tensor_tensor(
                        out=XT[b][:], in0=XT[b][:], scalar=RS[:, c:c + 1], in1=FG[:], op0=ALU.mult, op1=ALU.mult),
                       r=["XT%d" % b, ("RS", c), "FG"], w=["XT%d" % b])
                    DMA(y_d[s][c * 128:(c + 1) * 128, :], XT[b][:], r=["XT%d" % b], w=[("Y", s, c)])
                    if nxt == "seq":
                        xn, xnk = stats_scale(XT[2 + b], "XT%d" % (2 + b), NCH + c, n)
                        evs[n] = tpose(xn, [xnk], n)

            def o_evac(n):
                if n in evs:
                    evs.pop(n)()
            run_stages([(o_mm, 0), (o_chain, 1), (o_dma, -1), (o_evac, 2)], 0, NCH)
            if nxt == "layer":
                prefetch_next(l + 1)
            elif nxt == "seq":
                prefetch_next(0)

        for s in range(n_seq):
            for l in range(depth):
                nxt = "layer" if l < depth - 1 else ("seq" if s < n_seq - 1 else None)
                unit(s, l, l == depth - 1, s == 0 and l == 0, nxt)
        OP("sp", lambda e: e.nop(), r=[("Y", s, c) for s in range(n_seq) for c in range(NCH)])
        S.analyze()
        S.emit(nc, sems, dsems)
    return nc


def _host_layout(inputs, tables):
    f = lambda a: np.ascontiguousarray(np.asarray(a, dtype=np.float32))
    norm_g = f(inputs["norm_g"])
    common = {
        "w_in": f(inputs["w_in"]),
        "w_out": f(inputs["w_out"]),
        "normg_t": np.ascontiguousarray(norm_g.reshape(2, 8, 128).transpose(0, 2, 1)),
        "bf3": np.ascontiguousarray(np.repeat(f(inputs["b_f"]), 3, axis=1).reshape(2, 18, 1)),
        "fg_rep": np.ascontiguousarray(np.broadcast_to(f(inputs["final_g"])[None, :], (128, D))),
        "posk_t": np.ascontiguousarray(f(inputs["cmp_pos_k"]).transpose(0, 2, 1)),
        "posv_t": np.ascontiguousarray(f(inputs["cmp_pos_v"]).transpose(0, 2, 1)),
        "w1k_t": np.ascontiguousarray(f(inputs["cmp_w1_k"]).reshape(2, 32, 64, 64).transpose(0, 2, 1, 3).reshape(2, 64, 2048)),
        "w1v_t": np.ascontiguousarray(f(inputs["cmp_w1_v"]).reshape(2, 32, 64, 64).transpose(0, 2, 1, 3).reshape(2, 64, 2048)),
        "w2k": f(inputs["cmp_w2_k"]),
        "w2v": f(inputs["cmp_w2_v"]),
    }
    for k, v in tables.items():
        common["t_" + k] = np.ascontiguousarray(v)
    return common


def kernel(x, norm_g, w_in, b_f, cmp_pos_k, cmp_w1_k, cmp_w2_k, cmp_pos_v, cmp_w1_v, cmp_w2_v, w_out, final_g):
    inputs = dict(x=x, norm_g=norm_g, w_in=w_in, b_f=b_f, cmp_pos_k=cmp_pos_k, cmp_w1_k=cmp_w1_k, cmp_w2_k=cmp_w2_k,
                  cmp_pos_v=cmp_pos_v, cmp_w1_v=cmp_w1_v, cmp_w2_v=cmp_w2_v, w_out=w_out, final_g=final_g)
    n_cores = 8
    tables = _tables()
    common = _host_layout(inputs, tables)
    xs = np.ascontiguousarray(np.asarray(x, dtype=np.float32))
    per = xs.shape[0] // n_cores
    nc = build_nc(n_seq=per, depth=2, tables=tables)
    in_maps = []
    for c in range(n_cores):
        m = dict(common)
        m["x"] = xs[c * per:(c + 1) * per]
        in_maps.append(m)
    res = run_bass_kernel_spmd(nc, in_maps, core_ids=list(range(n_cores)))
    return np.concatenate([np.asarray(r["y"], dtype=np.float32) for r in res.results], axis=0)
```

```python
import os
import numpy as np
import ml_dtypes
from contextlib import ExitStack
import concourse.bass as bass
import concourse.mybir as mybir
from concourse.bass_utils import run_bass_kernel_spmd

F32 = mybir.dt.float32
BF16 = mybir.dt.bfloat16
AF = mybir.ActivationFunctionType
ALU = mybir.AluOpType

S_LEN = 2048
D = 1024
NCH = 16
D_IN = 3986
NEG = -30000.0
EPS = 1e-6
O_FQ, O_FK, O_FV, O_FF, O_FZ = 0, 384, 768, 1152, 1158
O_SQ, O_SK, O_SV, O_SZ = 1542, 1926, 2310, 2694
O_NQ, O_NKC, O_NVC, O_NKS, O_NVS, O_NKW, O_NVW, O_NG, O_NZ = 3078, 3334, 3398, 3462, 3526, 3590, 3654, 3718, 3730

ENGS = ("sp", "act", "pe", "dve", "pool")
PSUM_KEYS = frozenset(["PA0", "PA1", "PB0", "PB1", "PC", "PD", "PT1", "PT2"])
N_DMA_SEMS = 8


class Sched:
    def __init__(self):
        self.ops = []
        self.barriers = []

    def op(self, eng, fn, reads=(), writes=(), dma=False):
        self.ops.append(dict(eng=eng, fn=fn, reads=tuple(reads), writes=tuple(writes), dma=dma))

    def barrier(self):
        self.ops.append(dict(pseudo="barrier"))

    def mark(self):
        self.ops.append(dict(pseudo="mark"))

    def need(self, tag):
        self.ops.append(dict(pseudo="need", tag=tag))

    def done(self, tag):
        self.ops.append(dict(pseudo="done", tag=tag))

    def capture_begin(self):
        self._saved = self.ops
        self.ops = []

    def capture_end(self):
        cap, self.ops = self.ops, self._saved
        return cap

    @staticmethod
    def merge(a, b):
        def segs(lst):
            out, cur = [], []
            for o in lst:
                if o.get("pseudo") == "mark":
                    if cur:
                        out.append(cur)
                    cur = []
                else:
                    cur.append(o)
            if cur:
                out.append(cur)
            return out
        sa, sb_ = segs(a), segs(b)
        res, ia, ib = [], 0, 0
        done = set()

        def emit_a():
            nonlocal ia
            for o in sa[ia]:
                if o.get("pseudo") == "done":
                    done.add(o["tag"])
            res.extend(sa[ia])
            ia += 1
        while ia < len(sa) or ib < len(sb_):
            if ib < len(sb_):
                needs = [o["tag"] for o in sb_[ib] if o.get("pseudo") == "need" and o["tag"] not in done]
                if needs and ia < len(sa):
                    emit_a()
                    continue
            if ib >= len(sb_) or (ia < len(sa) and ia * len(sb_) <= ib * len(sa)):
                emit_a()
            else:
                res.extend(sb_[ib])
                ib += 1
        return res

    def finalize(self):
        ops, self.barriers = [], []
        for o in self.ops:
            p = o.get("pseudo")
            if p == "barrier":
                self.barriers.append(len(ops))
            elif p is not None:
                continue
            else:
                ops.append(o)
        self.ops = ops

    def analyze(self):
        self.finalize()
        ops = self.ops
        last_w, readers = {}, {}
        for i, o in enumerate(ops):
            deps = set()
            for r in o["reads"]:
                if r in last_w:
                    deps.add(last_w[r])
                if r in PSUM_KEYS:
                    for rd in readers.get(r, ()):
                        if ops[rd]["eng"] != o["eng"]:
                            deps.add(rd)
            for w in o["writes"]:
                if w in last_w:
                    deps.add(last_w[w])
                deps.update(readers.get(w, ()))
            deps.discard(i)
            o["deps"] = deps
            for r in o["reads"]:
                readers.setdefault(r, []).append(i)
            for w in o["writes"]:
                last_w[w] = i
                readers[w] = []
        for b in self.barriers:
            if b == 0 or b >= len(ops):
                continue
            lasts = {}
            dmas = []
            for i in range(b - 1, -1, -1):
                o = ops[i]
                if o["dma"]:
                    if len(dmas) < N_DMA_SEMS:
                        dmas.append(i)
                elif o["eng"] not in lasts:
                    lasts[o["eng"]] = i
                if len(lasts) == len(ENGS) and len(dmas) >= N_DMA_SEMS:
                    break
            seen = set()
            for i in range(b, len(ops)):
                e = ops[i]["eng"]
                if e in seen:
                    continue
                seen.add(e)
                ops[i]["deps"].update(lasts.values())
                ops[i]["deps"].update(dmas)
                if len(seen) == len(ENGS):
                    break
        observed = [False] * len(ops)
        for i, o in enumerate(ops):
            kept = set()
            for d in o["deps"]:
                od = ops[d]
                if od["eng"] == "pe" and o["eng"] == "pe" and not od["dma"] and not o["dma"]:
                    continue
                kept.add(d)
            o["deps"] = kept
        dma_idx = [i for i, o in enumerate(ops) if o["dma"]]
        for n, i in enumerate(dma_idx):
            ops[i]["dma_n"] = n
            if n >= N_DMA_SEMS:
                ops[i]["deps"].add(dma_idx[n - N_DMA_SEMS])
        for o in ops:
            for d in o["deps"]:
                observed[d] = True
        cnt = {e: 0 for e in ENGS}
        dcnt = [0] * N_DMA_SEMS
        for i, o in enumerate(ops):
            if o["dma"]:
                s = o["dma_n"] % N_DMA_SEMS
                dcnt[s] += 16
                o["sem"], o["val"], o["inc"] = ("dma", s), dcnt[s], True
            else:
                if observed[i]:
                    cnt[o["eng"]] += 1
                o["inc"] = observed[i]
                o["sem"], o["val"] = ("eng", o["eng"]), cnt[o["eng"]]

    def emit(self, nc, sems, dma_sems):
        ops = self.ops
        streams = {e: [i for i, o in enumerate(ops) if o["eng"] == e] for e in ENGS}

        def semof(key):
            return dma_sems[key[1]] if key[0] == "dma" else sems[key[1]]

        def run(ename, eng):
            seen = {}
            for i in streams[ename]:
                o = ops[i]
                need = {}
                for d in o["deps"]:
                    od = ops[d]
                    need[od["sem"]] = max(need.get(od["sem"], 0), od["val"])
                for k, v in need.items():
                    if seen.get(k, 0) >= v:
                        continue
                    eng.wait_ge(semof(k), v)
                    seen[k] = v
                ins = o["fn"](eng)
                if o["inc"]:
                    ins.then_inc(semof(o["sem"]), 16 if o["dma"] else 1)

        with nc.Block() as block:
            @block.sync
            def _(e):
                run("sp", e)

            @block.scalar
            def _(e):
                run("act", e)

            @block.tensor
            def _(e):
                run("pe", e)

            @block.vector
            def _(e):
                run("dve", e)

            @block.gpsimd
            def _(e):
                run("pool", e)


def _tables():
    bf = ml_dtypes.bfloat16
    t = {}
    t["ident"] = np.eye(128, dtype=np.float32).astype(bf)
    a = np.arange(128)[:, None]
    b = np.arange(128)[None, :]
    t["mdle4"] = np.tile(np.where(a <= b, 0.0, NEG), (1, 4)).astype(bf)
    t["mdfar4"] = np.tile(np.where(a > b, 0.0, NEG), (1, 4)).astype(bf)
    t["msb"] = np.where(b < a, 0.0, NEG).astype(bf)
    t["msbr"] = np.where(a + b > 127, 0.0, NEG).astype(bf)
    t["jm"] = np.ascontiguousarray(np.eye(128, dtype=np.float32)[::-1]).astype(bf)
    slopes = 2.0 ** (-8.0 * np.arange(1, 5) / 4.0)
    tpos = np.arange(S_LEN)
    nq = np.zeros((64, 4, S_LEN), np.float32)
    for g in range(4):
        nq[32, g] = slopes[g]
        nq[33, g] = slopes[g]
        nq[34, g] = -slopes[g] * 128.0 * (tpos // 128)
        nq[35, g] = -slopes[g] * (tpos % 128)
    t["nq_init"] = nq.reshape(64, 4 * S_LEN).astype(bf)
    kk = np.zeros((64, S_LEN), np.float32)
    kk[32] = tpos % 128
    kk[33] = 128.0 * (tpos // 128)
    kk[34] = 1.0
    kk[35] = 1.0
    t["nkw_init"] = kk.astype(bf)
    ks = kk.copy()
    for j in range(32):
        ks[j, 64 * j:64 * j + 64] = 1.0
    t["nks_init"] = ks.astype(bf)
    cpos = 16 * np.arange(127) + 31
    kc = np.zeros((64, 128), np.float32)
    kc[32, :127] = cpos % 128
    kc[33, :127] = 128.0 * (cpos // 128)
    kc[34, :127] = 1.0
    kc[35, :127] = 1.0
    t["kc_init"] = kc.astype(bf)
    bb = np.zeros((9, 256), np.float32)
    for k in range(9):
        bb[k, k + 126] = 1.0
    t["bigband"] = bb.astype(bf)
    mp = np.zeros((9, 128), np.float32)
    for r in range(9):
        mp[r, :] = np.where(np.arange(128) >= 16 * r - 1, 0.0, NEG)
    t["mpat"] = np.tile(mp, (1, 4)).astype(bf)
    cs_, ce_ = 16 * np.arange(127), 16 * np.arange(127) + 31
    ss_, se_ = 64 * np.arange(32), 64 * np.arange(32) + 63
    c2s = ((cs_[:, None] <= se_[None, :]) & (ce_[:, None] >= ss_[None, :])).astype(np.float32)
    vci = np.zeros((128, 33), np.float32)
    vci[:127, 0] = 1.0
    vci[:127, 1:] = c2s
    t["vc_init"] = vci.astype(bf)
    addm = np.zeros((128, 16, 32), np.float32)
    for i in range(16):
        cur = (128 * i + np.arange(128)) // 64
        back = cur[:, None] - np.arange(32)[None, :]
        valid = back >= 0
        forced = (np.arange(32)[None, :] == 0) | (valid & (back < 2))
        addm[:, i, :] = np.where(forced, 1e9, np.where(valid, 0.0, -1e9))
    t["addm"] = addm.reshape(128, 512)
    m3 = np.zeros((18, 3), np.float32)
    for p in range(18):
        m3[p, p % 3] = 1.0
    t["m3"] = m3
    return t


_TABLE_DT = {"addm": F32, "m3": F32}


def build_nc(n_seq=4, depth=2, tables=None, phases=("fox", "sb", "nsa", "out")):
    nc = bass.Bass("TRN2", target_bir_lowering=False)
    S = Sched()
    tables = tables or _tables()

    def dram(name, shape, dt, kind):
        return nc.dram_tensor(name, list(shape), dt, kind=kind).ap()

    x_d = dram("x", [n_seq, S_LEN, D], F32, "ExternalInput")
    y_d = dram("y", [n_seq, S_LEN, D], F32, "ExternalOutput")
    xs_d = dram("xs", [S_LEN, D], F32, "Internal")
    w_in_d = dram("w_in", [2, D, D_IN], F32, "ExternalInput")
    w_out_d = dram("w_out", [2, D, D], F32, "ExternalInput")
    normg_d = dram("normg_t", [2, 128, 8], F32, "ExternalInput")
    bf3_d = dram("bf3", [2, 18, 1], F32, "ExternalInput")
    fg_d = dram("fg_rep", [128, D], F32, "ExternalInput")
    posk_d = dram("posk_t", [2, 64, 32], F32, "ExternalInput")
    posv_d = dram("posv_t", [2, 64, 32], F32, "ExternalInput")
    w1k_d = dram("w1k_t", [2, 64, 2048], F32, "ExternalInput")
    w1v_d = dram("w1v_t", [2, 64, 2048], F32, "ExternalInput")
    w2k_d = dram("w2k", [2, 64, 64], F32, "ExternalInput")
    w2v_d = dram("w2v", [2, 64, 64], F32, "ExternalInput")
    tab_d = {k: dram("t_" + k, v.shape, _TABLE_DT.get(k, BF16), "ExternalInput") for k, v in tables.items()}

    es = ExitStack()
    with es:
        def sb(name, shape, dt):
            return es.enter_context(nc.sbuf_tensor(name, list(shape), dt))

        def ps(name, shape, dt):
            return es.enter_context(nc.psum_tensor(name, list(shape), dt))

        HT = sb("HT", [128, 8, S_LEN], BF16)
        M = sb("M", [128, NCH, D], BF16)
        WST = sb("WST", [128, 4, 512], F32)
        WB = [sb("WB%d" % i, [128, 8, 512], BF16) for i in range(2)]
        XT = [sb("XT%d" % i, [128, D], F32) for i in range(4)]
        XN = [sb("XN%d" % i, [128, D], BF16) for i in range(3)]
        PTL = [sb("PTL%d" % i, [128, 512], BF16) for i in range(4)]
        IDENT = sb("IDENT", [128, 128], BF16)
        MDLE4 = sb("MDLE4", [128, 512], BF16)
        MDFAR4 = sb("MDFAR4", [128, 512], BF16)
        MSB = sb("MSB", [128, 128], BF16)
        MSBR = sb("MSBR", [128, 128], BF16)
        JM = sb("JM", [128, 128], BF16)
        ADDM = sb("ADDM", [128, 16, 32], F32)
        ONESB = sb("ONESB", [128, 1024], BF16)
        BIGBAND = sb("BIGBAND", [9, 256], BF16)
        MPAT = sb("MPAT", [9, 512], BF16)
        M3 = sb("M3", [18, 3], F32)
        FG = sb("FG", [128, D], F32)
        G = sb("G", [128, 8], F32)
        SS = sb("SS", [128, 2 * NCH], F32)
        SQ_ = sb("SQ_", [128, 2 * NCH], F32)
        RS = sb("RS", [128, 2 * NCH], F32)
        VC = sb("VC", [128, 97], BF16)
        SMALL = sb("SMALL", [128, 256], F32)
        ARENA_BYTES = 73 * 1024
        ARENA = sb("ARENA", [128, ARENA_BYTES // 2], BF16)

        PA = ps("PA", [128, 1024], F32)
        PB = ps("PB", [128, 1024], F32)
        PC = ps("PC", [128, 512], F32)
        PD = ps("PD", [128, 512], F32)
        PT1 = ps("PT1", [128, 1024], BF16)
        PT2 = ps("PT2", [128, 1024], BF16)
        STB = [(PA[:, 0:512], "PA0"), (PA[:, 512:1024], "PA1"), (PB[:, 0:512], "PB0"), (PB[:, 512:1024], "PB1")]
        PTB = [(PT1, "PT1"), (PT2, "PT2")]
        OB = [(PC, "PC"), (PD, "PD")]

        sems = {e: es.enter_context(nc.semaphore("s_" + e)) for e in ENGS}
        dsems = [es.enter_context(nc.semaphore("d%d" % i)) for i in range(N_DMA_SEMS)]

        def carve(off, parts, nelem, dt):
            nb = nelem * (4 if dt == F32 else 2)
            assert off % 4 == 0 and off + nb <= ARENA_BYTES, (off, nb)
            ap = ARENA[0:parts, off // 2:(off + nb) // 2]
            return ap.bitcast(F32) if dt == F32 else ap

        def OP(eng, fn, r=(), w=(), dma=False):
            S.op(eng, fn, r, w, dma)

        def DMA(out, in_, r=(), w=()):
            OP("sp", lambda e, out=out, in_=in_: e.dma_start(out=out, in_=in_), r, w, dma=True)

        for nm, tile in [("ident", IDENT), ("mdle4", MDLE4), ("mdfar4", MDFAR4), ("msb", MSB), ("msbr", MSBR), ("jm", JM),
                         ("bigband", BIGBAND), ("mpat", MPAT), ("m3", M3)]:
            DMA(tile[:], tab_d[nm], w=[nm])
        DMA(ADDM[:].rearrange("p a b -> p (a b)"), tab_d["addm"], w=["addm"])
        DMA(FG[:], fg_d, w=["FG"])
        DMA(VC[:, 64:97], tab_d["vc_init"], w=["VCc"])
        OP("pool", lambda e: e.memset(ONESB[:], 1.0), w=["ONESB"])
        OP("pool", lambda e: e.memset(ARENA[:], 0.0), w=["ARENA0"])
        S.barrier()
        CONST = ["ident", "mdle4", "mdfar4", "msb", "bigband", "mpat", "m3", "addm", "FG", "VCc", "ONESB"]

        st_rr = [0]
        ptl_rr = [0]

        def next_st():
            st_rr[0] = (st_rr[0] + 1) % 4
            return STB[st_rr[0]]

        def next_ptl():
            ptl_rr[0] = (ptl_rr[0] + 1) % 4
            return PTL[ptl_rr[0]], "PTL%d" % ptl_rr[0]

        w_rr = [0]

        def wload(src3, specs, scale_g=True, buf=None, engines=("pool",)):
            if buf is None:
                w_rr[0] ^= 1
                buf = w_rr[0]
            wb = WB[buf]
            key = "WB%d" % buf
            tot = max(dc + n for (_, n, dc) in specs)
            for kh in range(2):
                stk = []
                for j, (sc, n, dc) in enumerate(specs):
                    DMA(WST[:, :, dc:dc + n], src3[:, 4 * kh:4 * kh + 4, sc:sc + n], w=[("WST", j)])
                    stk.append(("WST", j))
                for kk in range(4):
                    k = 4 * kh + kk
                    ce = engines[k % len(engines)]
                    if scale_g and ce == "act":
                        OP("act", lambda e, k=k, kk=kk, tot=tot, wb=wb: e.activation(
                            out=wb[:, k, 0:tot], in_=WST[:, kk, 0:tot], func=AF.Copy, scale=G[:, k:k + 1]),
                           r=stk + ["G"], w=[(key, k)])
                    elif scale_g and ce == "dve":
                        OP("dve", lambda e, k=k, kk=kk, tot=tot, wb=wb: e.tensor_scalar(
                            out=wb[:, k, 0:tot], in0=WST[:, kk, 0:tot], scalar1=G[:, k:k + 1], scalar2=None,
                            op0=ALU.mult), r=stk + ["G"], w=[(key, k)])
                    elif scale_g:
                        OP("pool", lambda e, k=k, kk=kk, tot=tot, wb=wb: e.tensor_scalar(
                            out=wb[:, k, 0:tot], in0=WST[:, kk, 0:tot], scalar1=G[:, k:k + 1], scalar2=0.0,
                            op0=ALU.mult, op1=ALU.add), r=stk + ["G"], w=[(key, k)])
                    else:
                        OP("pool", lambda e, k=k, kk=kk, tot=tot, wb=wb: e.tensor_copy(out=wb[:, k, 0:tot], in_=WST[:, kk, 0:tot]),
                           r=stk, w=[(key, k)])
            return wb, [(key, k) for k in range(8)]

        def proj_fm(wb, wkeys, c0, m, evac, banks=None, bank_rr=[0]):
            banks = banks or OB
            for tg in range(4):
                S.mark()
                bank_rr[0] += 1
                pb, pk = banks[bank_rr[0] % len(banks)]

                def f(e, tg=tg, pb=pb):
                    ins = None
                    for k in range(8):
                        ins = e.matmul(pb[0:m, 0:512], lhsT=wb[:, k, c0:c0 + m], rhs=HT[:, k, tg * 512:(tg + 1) * 512],
                                       start=(k == 0), stop=(k == 7))
                    return ins
                OP("pe", f, r=wkeys + [("HT", c) for c in range(tg * 4, tg * 4 + 4)], w=[pk])
                evac(tg, pb, pk)

        def proj_tm(wb, wkeys, c0, n, evac, banks=None, bank_rr=[0]):
            banks = banks or OB
            for c in range(NCH):
                if c % 2 == 0:
                    S.mark()
                bank_rr[0] += 1
                pb, pk = banks[bank_rr[0] % len(banks)]

                def f(e, c=c, pb=pb):
                    ins = None
                    for k in range(8):
                        ins = e.matmul(pb[:, 0:n], lhsT=HT[:, k, c * 128:(c + 1) * 128], rhs=wb[:, k, c0:c0 + n],
                                       start=(k == 0), stop=(k == 7))
                    return ins
                OP("pe", f, r=wkeys + [("HT", c)], w=[pk])
                evac(c, pb, pk)

        tp_rr = [0]

        def tpose(src_ap, src_keys, c):
            tp_rr[0] ^= 1
            pt, ptk = PTB[tp_rr[0]]

            def f(e, pt=pt):
                ins = None
                for k in range(8):
                    ins = e.transpose(out=pt[:, k * 128:(k + 1) * 128], in_=src_ap[:, k * 128:(k + 1) * 128], identity=IDENT[:])
                return ins
            OP("pe", f, r=list(src_keys) + ["ident"], w=[ptk])

            def evac():
                if c % 2 == 0:
                    OP("act", lambda e, pt=pt, c=c: e.activation(out=HT[:, :, c * 128:(c + 1) * 128],
                                                                 in_=pt[:].rearrange("p (k t) -> p k t", k=8), func=AF.Copy),
                       r=[ptk], w=[("HT", c)])
                else:
                    OP("dve", lambda e, pt=pt, c=c: e.tensor_copy(out=HT[:, :, c * 128:(c + 1) * 128],
                                                                  in_=pt[:].rearrange("p (k t) -> p k t", k=8)),
                       r=[ptk], w=[("HT", c)])
            return evac

        def transpose_into_HT(src_ap, src_keys, c):
            tpose(src_ap, src_keys, c)()

        def stats_scale(xt, xtk, col, n):
            xn, xnk = XN[n % 2], "XN%d" % (n % 2)
            OP("dve", lambda e, xt=xt, col=col: e.scalar_tensor_tensor(
                out=XN[2][:], in0=xt[:], scalar=1.0, in1=xt[:], op0=ALU.mult, op1=ALU.mult,
                accum_out=SS[:, col:col + 1]), r=[xtk], w=["XJ", ("SS", col)])
            OP("act", lambda e, col=col: e.activation(out=SQ_[:, col:col + 1], in_=SS[:, col:col + 1], func=AF.Sqrt,
                                                      bias=EPS, scale=1.0 / D), r=[("SS", col)], w=[("SQ", col)])
            OP("dve", lambda e, col=col: e.reciprocal(out=RS[:, col:col + 1], in_=SQ_[:, col:col + 1]), r=[("SQ", col)], w=[("RS", col)])
            OP("dve", lambda e, xt=xt, xn=xn, col=col: e.tensor_scalar(out=xn[:], in0=xt[:], scalar1=RS[:, col:col + 1],
                                                                         scalar2=None, op0=ALU.mult),
               r=[xtk, ("RS", col)], w=[xnk])
            return xn, xnk

        def run_stages(stages, lo, hi):
            offs = [o for _, o in stages]
            for k in range(lo + min(offs), hi + max(offs)):
                for fn, off in stages:
                    n = k - off
                    if lo <= n < hi:
                        fn(n)

        pref = {}

        def fox_specs(p):
            return [(O_FQ + 128 * p, 128, 0), (O_FK + 128 * p, 128, 128), (O_FV + 128 * p, 128, 256), (O_FZ + 128 * p, 128, 384)]

        def sb_specs(p):
            return [(O_SQ + 128 * p, 128, 0), (O_SK + 128 * p, 128, 128), (O_SV + 128 * p, 128, 256), (O_SZ + 128 * p, 128, 384)]

        def prefetch_next(l2):
            w3 = w_in_d[l2].rearrange("(k p) e -> p k e", p=128)
            DMA(G[:], normg_d[l2], w=["G"])
            pref["fox"] = wload(w3, fox_specs(0), buf=0, engines=("act", "dve", "pool"))
            pref["sb"] = wload(w3, sb_specs(0), buf=1, engines=("act", "dve", "pool"))

        def unit(s, l, last, do_phase_a, nxt):
            w_in3 = w_in_d[l].rearrange("(k p) e -> p k e", p=128)
            w_out3 = w_out_d[l].rearrange("(k p) e -> p k e", p=128)
            if not pref:
                DMA(G[:], normg_d[l], w=["G"])

            def xsrc(c):
                return (x_d[s] if l == 0 else xs_d)[c * 128:(c + 1) * 128, :]

            if do_phase_a:
                evs = {}

                def a_dma(n):
                    DMA(XT[n % 2][:], xsrc(n), r=[("XS", n)], w=["XT%d" % (n % 2)])

                def a_chain(n):
                    xn, xnk = stats_scale(XT[n % 2], "XT%d" % (n % 2), n, n)
                    evs[n] = tpose(xn, [xnk], n)

                def a_evac(n):
                    evs.pop(n)()
                run_stages([(a_chain, 0), (a_dma, -1), (a_evac, 1)], 0, NCH)

            HTALL = [("HT", c) for c in range(NCH)]

            FQ = carve(0, 128, 2048, BF16)
            FK = carve(4096, 128, 2048, BF16)
            VA = carve(8192, 128, NCH * 2 * 65, BF16).rearrange("p (c h d) -> p c h d", c=NCH, h=2)
            ZF = carve(12352, 128, NCH * 128, BF16).rearrange("p (c d) -> p c d", c=NCH)
            CT = carve(16448, 18, 2048, BF16)
            SBB = 22528
            CSC = SBB + 24576
            EF = carve(CSC, 18, 2048, F32)
            C32 = carve(CSC + 8192, 18, 2048, F32)
            HI = carve(CSC + 16384, 18, 2048, BF16)
            MID = carve(CSC + 20480, 18, 2048, BF16)
            WSTF = carve(20544, 128, 48, F32).rearrange("p (k c) -> p k c", k=8)
            WFF6 = carve(20736, 128, 48, BF16).rearrange("p (k c) -> p k c", k=8)
            WFF3 = carve(20832, 128, 144, BF16).rearrange("p (k c) -> p k c", k=8)
            BF3 = carve(21120, 18, 1, F32)
            NB3 = carve(21124, 18, 1, F32)
            RD = carve(21128, 128, 4, F32)

            DMA(WSTF, w_in3[:, :, O_FF:O_FF + 6], w=["WSTF"])
            DMA(BF3, bf3_d[l], w=["BF3"])
            OP("dve", lambda e: e.tensor_scalar(out=NB3, in0=BF3, scalar1=-1.0, scalar2=None, op0=ALU.mult),
               r=["BF3"], w=["NB3"])
            for k in range(8):
                OP("pool", lambda e, k=k: e.tensor_scalar(out=WFF6[:, k, :], in0=WSTF[:, k, :], scalar1=G[:, k:k + 1],
                                                          scalar2=0.0, op0=ALU.mult, op1=ALU.add),
                   r=["WSTF", "G"], w=[("WFF6", k)])
            for j in range(3):
                OP("pool", lambda e, j=j: e.tensor_copy(out=WFF3[:, :, j:18:3], in_=WFF6[:, :, :]),
                   r=[("WFF6", k) for k in range(8)], w=[("WFF3", j)])
            def cprep1():
                for tg in range(4):
                    pb, pk = FOXPB[tg % 3]

                    def f(e, tg=tg, pb=pb):
                        ins = None
                        for k in range(8):
                            ins = e.matmul(pb[0:18, 0:512], lhsT=WFF3[:, k, :], rhs=HT[:, k, tg * 512:(tg + 1) * 512],
                                           start=(k == 0), stop=(k == 7))
                        return ins
                    OP("pe", f, r=[("WFF3", j) for j in range(3)] + HTALL, w=[pk])
                    OP("act", lambda e, tg=tg, pb=pb: e.activation(out=EF[:, tg * 512:(tg + 1) * 512], in_=pb[0:18, 0:512],
                                                                    func=AF.Exp, bias=NB3, scale=-1.0),
                       r=[pk, "NB3"], w=[("EF", tg)])
                OP("act", lambda e: e.activation(out=EF, in_=EF, func=AF.Ln, bias=1.0, scale=1.0),
                   r=[("EF", tg) for tg in range(4)], w=["SPF"])

            def cprep2():
                S.mark()
                OP("dve", lambda e: e.tensor_tensor_scan(out=C32[:, 0:1024], data0=ONESB[0:18, 0:1024], data1=EF[:, 0:1024],
                                                         initial=0.0, op0=ALU.mult, op1=ALU.subtract),
                   r=["SPF", "ONESB"], w=["C32a"])
                S.mark()
                OP("dve", lambda e: e.tensor_tensor_scan(out=C32[:, 1024:2048], data0=ONESB[0:18, 0:1024], data1=EF[:, 1024:2048],
                                                         initial=C32[:, 1023:1024], op0=ALU.mult, op1=ALU.subtract),
                   r=["SPF", "ONESB", "C32a"], w=["C32b"])
                S.mark()
                OP("dve", lambda e: e.tensor_copy(out=HI, in_=C32), r=["C32a", "C32b"], w=["HI"])
                S.mark()
                OP("dve", lambda e: e.tensor_tensor(out=EF, in0=C32, in1=HI, op=ALU.subtract), r=["C32a", "C32b", "HI", "SPF"], w=["R1", "SPF"])
                S.mark()
                OP("dve", lambda e: e.tensor_copy(out=MID, in_=EF), r=["R1"], w=["MID"])
                S.mark()
                OP("dve", lambda e: e.tensor_tensor(out=C32, in0=EF, in1=MID, op=ALU.subtract), r=["R1", "MID", "HI"], w=["R2", "C32a", "C32b"])
                S.mark()
                OP("dve", lambda e: e.tensor_scalar(out=CT, in0=HI, scalar1=M3[:, 0:1], scalar2=None, op0=ALU.mult),
                   r=["HI", "m3"], w=["CT0"])
                S.mark()
                OP("dve", lambda e: e.scalar_tensor_tensor(out=CT, in0=MID, scalar=M3[:, 1:2], in1=CT, op0=ALU.mult, op1=ALU.add),
                   r=["MID", "CT0"], w=["CT1"])
                S.mark()
                OP("dve", lambda e: e.scalar_tensor_tensor(out=CT, in0=C32, scalar=M3[:, 2:3], in1=CT, op0=ALU.mult, op1=ALU.add),
                   r=["R2", "CT1"], w=["CT"])


            OP("pool", lambda e: e.memset(FK[64:70, :], 1.0), w=["FKc"])
            OP("pool", lambda e: e.memset(FQ[64:70, :], -1.0), w=["FQc"])
            OP("pool", lambda e: e.memset(VA, 1.0), w=["VA1"])

            FOXB = [OB[0]]
            SBBK = [OB[1]]
            FOXPB = [OB[0], STB[0], STB[1]]
            SBPB = [OB[1], STB[2], STB[3]]
            fox_st_rr = [0]

            def fox_st():
                fox_st_rr[0] ^= 1
                return STB[fox_st_rr[0]]
            S.capture_begin()
            for p in range(3):
                if p == 0 and "fox" in pref:
                    wb, wk = pref.pop("fox")
                else:
                    wb, wk = wload(w_in3, fox_specs(p), buf=0)

                def ev_tm(c, pb, pk):
                    OP("dve", lambda e, c=c, pb=pb: e.tensor_copy(out=VA[:, c, :, 0:64],
                                                                   in_=pb[:, 0:128].rearrange("p (h d) -> p h d", h=2)),
                       r=[pk, "VA1"], w=[("VA", c)])
                    OP("act", lambda e, c=c, pb=pb: e.activation(out=ZF[:, c, :], in_=pb[:, 128:256], func=AF.Silu),
                       r=[pk], w=[("ZF", c)])
                proj_tm(wb, wk, 256, 256, ev_tm, banks=FOXPB)
                if p == 0:
                    S.mark()
                    cprep1()
                for hh in range(2):
                    h = 2 * p + hh

                    def ev_q(tg, pb, pk):
                        OP("act", lambda e, tg=tg, pb=pb: e.activation(out=FQ[0:64, tg * 512:(tg + 1) * 512],
                                                                        in_=pb[0:64, 0:512], func=AF.Copy, scale=0.125),
                           r=[pk], w=[("FQ", tg)])

                    def ev_k(tg, pb, pk):
                        OP("dve", lambda e, tg=tg, pb=pb: e.tensor_copy(out=FK[0:64, tg * 512:(tg + 1) * 512], in_=pb[0:64, 0:512]),
                           r=[pk], w=[("FK", tg)])
                    proj_fm(wb, wk, hh * 64, 64, ev_q, banks=FOXPB)
                    proj_fm(wb, wk, 128 + hh * 64, 64, ev_k, banks=FOXPB)
                    if h == 0:
                        S.mark()
                        cprep2()
                        S.done("CT")
                    if h == 5:
                        pref["n1"] = wload(w_in3, [(O_NKC, 64, 0), (O_NQ, 256, 64), (O_NVC, 64, 320), (O_NKS, 64, 384), (O_NKW, 64, 448)], buf=0)
                    DMA(FK[64:67, :], CT[3 * h:3 * h + 3, :], r=["CT", "FKc"], w=["FKa"])
                    DMA(FQ[67:70, :], CT[3 * h:3 * h + 3, :], r=["CT", "FQc"], w=["FQa"])
                    FQK = [("FQ", tg) for tg in range(4)] + [("FK", tg) for tg in range(4)] + ["FKa", "FQa", "FKc", "FQc"]

                    tiles = []
                    for g in range(4):
                        for cs in range(4 * g + 4):
                            tiles.append((g, cs))
                    state = {}

                    def stA(n):
                        g, cs = tiles[n]
                        r = max(0, cs - 4 * g)
                        c0 = 128 * r
                        st, stk = fox_st()
                        ptl, ptk = next_ptl()
                        state[n] = (st, stk, ptl, ptk, r, c0)

                        def f(e, g=g, cs=cs, st=st, c0=c0, r=r):
                            ins = e.matmul(st[:, c0:512], lhsT=FK[0:70, cs * 128:(cs + 1) * 128],
                                           rhs=FQ[0:70, g * 512 + c0:(g + 1) * 512], start=True, stop=(cs < 4 * g))
                            if cs >= 4 * g:
                                ins = e.matmul(st[:, c0:c0 + 128], lhsT=IDENT[:], rhs=MDLE4[:, 0:128], start=False, stop=True)
                            return ins
                        OP("pe", f, r=FQK + ["ident", "mdle4"], w=[stk])
                        OP("act", lambda e, st=st, ptl=ptl, c0=c0: e.activation(out=ptl[:, c0:512], in_=st[:, c0:512], func=AF.Exp),
                           r=[stk], w=[ptk])

                    def stC(n):
                        g, cs = tiles[n]
                        st, stk, ptl, ptk, r, c0 = state.pop(n)
                        ob, obk = FOXB[0]

                        def f(e, g=g, cs=cs, ptl=ptl, r=r, ob=ob, hh=hh):
                            ins = None
                            for tb in range(r, 4):
                                ins = e.matmul(ob[:, tb * 65:(tb + 1) * 65], lhsT=ptl[:, tb * 128:(tb + 1) * 128],
                                               rhs=VA[:, cs, hh, :], start=(cs == 0 and tb == 0), stop=(cs == 4 * g + tb),
                                               skip_group_check=True)
                            return ins
                        OP("pe", f, r=[ptk, ("VA", cs)], w=[obk])
                        if cs == 4 * g + 3:
                            OP("dve", lambda e, ob=ob: e.reciprocal(
                                out=RD[:, 0:4], in_=ob[:, 0:260].rearrange("p (t d) -> p t d", d=65)[:, :, 64]),
                               r=[obk], w=["RD"])
                            for tb in range(4):
                                c = 4 * g + tb
                                OP("dve", lambda e, ob=ob, tb=tb, c=c, h=h, hh=hh: e.scalar_tensor_tensor(
                                    out=M[:, c, h * 64:(h + 1) * 64], in0=ob[:, tb * 65:tb * 65 + 64], scalar=RD[:, tb:tb + 1],
                                    in1=ZF[:, c, hh * 64:(hh + 1) * 64], op0=ALU.mult, op1=ALU.mult),
                                   r=[obk, "RD", ("ZF", c)], w=[("M", c, h)])

                    DEPTH = 2
                    for m in range(len(tiles) + DEPTH):
                        S.mark()
                        if m - DEPTH >= 0:
                            stC(m - DEPTH)
                        if m < len(tiles):
                            stA(m)
            fox_ops = S.capture_end()

            S.capture_begin()
            SQt = carve(SBB + 0, 128, 2048, BF16)
            SKR = carve(SBB + 4096, 128, 2048, BF16)
            VR = carve(SBB + 8192, 128, NCH * 128, BF16).rearrange("p (c d) -> p c d", c=NCH)
            ZS = carve(SBB + 12288, 128, NCH * 128, BF16).rearrange("p (c d) -> p c d", c=NCH)
            KTM = carve(SBB + 16384, 128, NCH * 128, BF16).rearrange("p (c d) -> p c d", c=NCH)
            VTM = carve(SBB + 20480, 128, NCH * 128, BF16).rearrange("p (c d) -> p c d", c=NCH)
            EB = [carve(SBB + 24576 + 2048 * j, 128, 512, F32) for j in range(4)]
            LPB = [carve(SBB + 32768 + 2048 * j, 128, 512, F32) for j in range(2)]
            GX = [carve(SBB + 36864 + 2056 * j, 128, 514, F32) for j in range(3)]
            AB = [carve(SBB + 47104 + 1024 * j, 128, 512, BF16) for j in range(2)]
            ATB = [carve(SBB + 49152 + 1024 * j, 128, 512, BF16) for j in range(2)]
            sb_st_rr = [0]

            def sb_st():
                sb_st_rr[0] ^= 1
                return STB[2 + sb_st_rr[0]]

            for p in range(3):
                if p == 0 and "sb" in pref:
                    wb, wk = pref.pop("sb")
                else:
                    wb, wk = wload(w_in3, sb_specs(p), buf=1)

                def ev_tm(c, pb, pk):
                    OP("dve", lambda e, c=c, pb=pb: e.tensor_copy(out=KTM[:, c, :], in_=pb[:, 0:128]), r=[pk], w=[("KTM", c)])
                    OP("dve", lambda e, c=c, pb=pb: e.tensor_copy(out=VTM[:, c, :], in_=pb[:, 128:256]), r=[pk], w=[("VTM", c)])
                    OP("act", lambda e, c=c, pb=pb: e.activation(out=ZS[:, c, :], in_=pb[:, 256:384], func=AF.Silu),
                       r=[pk], w=[("ZS", c)])
                proj_tm(wb, wk, 128, 384, ev_tm, banks=SBPB)

                def ev_q(tg, pb, pk):
                    OP("act", lambda e, tg=tg, pb=pb: e.activation(out=SQt[:, tg * 512:(tg + 1) * 512], in_=pb[:, 0:512],
                                                                    func=AF.Copy, scale=0.125), r=[pk], w=[("SQ", tg)])
                proj_fm(wb, wk, 0, 128, ev_q, banks=SBPB)
                if p == 2:
                    pref["n2"] = wload(w_in3, [(O_NVS, 64, 0), (O_NVW, 64, 64), (O_NG, 12, 128), (O_NZ, 256, 140)], buf=1)
                for c0 in range(0, NCH, 4):
                    S.mark()
                    pbk, pkk = SBBK[0]
                    pbv, pkv = SBBK[0]

                    def fk(e, c0=c0, pbk=pbk):
                        ins = None
                        for c in range(c0, c0 + 4):
                            sl = 3 - (c - c0)
                            ins = e.matmul(pbk[:, sl * 128:(sl + 1) * 128], lhsT=KTM[:, c, :], rhs=JM[:], start=True, stop=True)
                        return ins
                    OP("pe", fk, r=[("KTM", c) for c in range(c0, c0 + 4)] + ["jm"], w=[pkk])
                    q0 = (15 - c0 - 3) * 128
                    OP("act", lambda e, pbk=pbk, q0=q0: e.activation(out=SKR[:, q0:q0 + 512], in_=pbk[:, 0:512], func=AF.Copy),
                       r=[pkk], w=[("SKR", (15 - c0 - 3) // 4)])

                    def fv(e, c0=c0, pbv=pbv):
                        ins = None
                        for c in range(c0, c0 + 4):
                            sl = 3 - (c - c0)
                            ins = e.matmul(pbv[:, sl * 128:(sl + 1) * 128], lhsT=JM[:], rhs=VTM[:, c, :], start=True, stop=True)
                        return ins
                    OP("pe", fv, r=[("VTM", c) for c in range(c0, c0 + 4)] + ["jm"], w=[pkv])
                    OP("dve", lambda e, pbv=pbv, c0=c0: e.tensor_copy(
                        out=VR[:, 15 - c0 - 3:16 - c0, :], in_=pbv[:, 0:512].rearrange("p (c d) -> p c d", c=4)),
                       r=[pkv], w=[("VR", (15 - c0 - 3) // 4)])
                SQK = [("SQ", tg) for tg in range(4)] + [("SKR", tg) for tg in range(4)]

                tiles = []
                for hh in range(2):
                    for i in range(NCH):
                        k0 = 128 * (15 - i)
                        j = 0
                        while k0 < 2048:
                            k1 = min(k0 + 512, 2048)
                            tiles.append((hh, i, j, k0, k1))
                            k0 = k1
                            j += 1
                NT = len(tiles)
                stn = {}

                def s0(n):
                    hh, i, j, k0, k1 = tiles[n]
                    st, stk = sb_st()
                    stn[n] = (st, stk)
                    nn = k1 - k0

                    def f(e, hh=hh, i=i, j=j, k0=k0, k1=k1, st=st, nn=nn):
                        ins = e.matmul(st[:, 0:nn], lhsT=SQt[hh * 64:(hh + 1) * 64, i * 128:(i + 1) * 128],
                                       rhs=SKR[hh * 64:(hh + 1) * 64, k0:k1], start=True, stop=(j != 0))
                        if j == 0:
                            ins = e.matmul(st[:, 0:128], lhsT=IDENT[:], rhs=MSBR[:], start=False, stop=True)
                        return ins
                    OP("pe", f, r=SQK + ["ident", "msbr"], w=[stk])

                def s1(n):
                    hh, i, j, k0, k1 = tiles[n]
                    st, stk = stn.pop(n)
                    nn = k1 - k0
                    eb = EB[n % 4]
                    OP("act", lambda e, st=st, eb=eb, nn=nn: e.activation(out=eb[:, 0:nn], in_=st[:, 0:nn], func=AF.Tanh, scale=-0.5),
                       r=[stk, "CT"], w=[("EB", n % 4)])
                    OP("act", lambda e, eb=eb, nn=nn: e.activation(out=eb[:, 0:nn], in_=eb[:, 0:nn], func=AF.Copy, bias=0.5, scale=0.5),
                       r=[("EB", n % 4), "CT"], w=[("EB", n % 4)])

                def s2(n):
                    hh, i, j, k0, k1 = tiles[n]
                    nn = k1 - k0
                    eb, gx = EB[n % 4], GX[n % 3]
                    if j == 0:
                        init, rk = 1.0, []
                        OP("act", lambda e, gx=gx: e.activation(out=gx[:, 0:1], in_=ONESB[:, 0:1], func=AF.Copy), r=["CT", "ONESB"], w=[("GX0", n % 3)])
                    else:
                        pn = tiles[n - 1][4] - tiles[n - 1][3]
                        pgx = GX[(n - 1) % 3]
                        init, rk = pgx[:, pn:pn + 1], [("GX", (n - 1) % 3)]
                        OP("act", lambda e, gx=gx, pgx=pgx, pn=pn: e.activation(out=gx[:, 0:1], in_=pgx[:, pn:pn + 1], func=AF.Copy),
                           r=rk + ["CT"], w=[("GX0", n % 3)])
                    OP("dve", lambda e, eb=eb, gx=gx, nn=nn, init=init: e.tensor_tensor_scan(
                        out=gx[:, 1:nn + 1], data0=eb[:, 0:nn], data1=ONESB[:, 0:nn], initial=init, op0=ALU.mult, op1=ALU.mult),
                       r=[("EB", n % 4), "ONESB", "CT"] + rk, w=[("GX", n % 3)])

                def s3(n):
                    pass

                def s4(n):
                    hh, i, j, k0, k1 = tiles[n]
                    nn = k1 - k0
                    gx, ab = GX[n % 3], AB[n % 2]
                    OP("pool", lambda e, gx=gx, ab=ab, nn=nn: e.tensor_tensor(out=ab[:, 0:nn], in0=gx[:, 0:nn], in1=gx[:, 1:nn + 1], op=ALU.subtract),
                       r=[("GX", n % 3), ("GX0", n % 3), "CT"], w=[("AB", n % 2)])

                def s5(n):
                    hh, i, j, k0, k1 = tiles[n]
                    nn = k1 - k0
                    ab = AB[n % 2]
                    pt, ptk = PTB[n % 2]

                    def f(e, ab=ab, pt=pt, nn=nn):
                        ins = None
                        for blk in range(nn // 128):
                            ins = e.transpose(out=pt[:, blk * 128:(blk + 1) * 128], in_=ab[:, blk * 128:(blk + 1) * 128], identity=IDENT[:])
                        return ins
                    OP("pe", f, r=[("AB", n % 2), "ident"], w=[ptk])

                def s6(n):
                    hh, i, j, k0, k1 = tiles[n]
                    nn = k1 - k0
                    pt, ptk = PTB[n % 2]
                    atb = ATB[n % 2]
                    OP("dve", lambda e, pt=pt, atb=atb, nn=nn: e.tensor_copy(out=atb[:, 0:nn], in_=pt[:, 0:nn]),
                       r=[ptk, "CT"], w=[("ATB", n % 2)])

                def s7(n):
                    hh, i, j, k0, k1 = tiles[n]
                    nn = k1 - k0
                    atb = ATB[n % 2]
                    ob, obk = SBBK[0]
                    h = 2 * p + hh
                    last = (k1 == 2048)

                    def f(e, atb=atb, ob=ob, hh=hh, j=j, k0=k0, nn=nn, last=last):
                        ins = None
                        nblk = nn // 128
                        for blk in range(nblk):
                            ins = e.matmul(ob[:, 0:64], lhsT=atb[:, blk * 128:(blk + 1) * 128],
                                           rhs=VR[:, k0 // 128 + blk, hh * 64:(hh + 1) * 64],
                                           start=(j == 0 and blk == 0), stop=(last and blk == nblk - 1))
                        return ins
                    OP("pe", f, r=[("ATB", n % 2)] + [("VR", q) for q in range(4)], w=[obk])
                    if last:
                        OP("dve", lambda e, ob=ob, i=i, h=h, hh=hh: e.tensor_tensor(
                            out=M[:, i, 384 + h * 64:384 + (h + 1) * 64], in0=ob[:, 0:64], in1=ZS[:, i, hh * 64:(hh + 1) * 64],
                            op=ALU.mult), r=[obk, ("ZS", i)], w=[("M", i, 6 + h)])

                stages = [s0, s1, s2, s3, s4, s5, s6, s7]
                for k in range(NT + len(stages)):
                    S.mark()
                    if p == 0 and k == 0:
                        S.need("CT")
                    for sidx in reversed(range(len(stages))):
                        n = k - sidx
                        if 0 <= n < NT:
                            stages[sidx](n)
            sb_ops = S.capture_end()
            merged = Sched.merge(fox_ops, sb_ops)
            ct_idx = max(i for i, o in enumerate(merged) if "CT" in o.get("writes", ()))
            eb_idx = min(i for i, o in enumerate(merged) if any(isinstance(w, tuple) and w[0] in ("EB", "GX", "GX0", "AB", "ATB")
                                                                for w in o.get("writes", ())))
            assert ct_idx < eb_idx, (ct_idx, eb_idx)
            S.ops.extend(merged)
            S.barrier()

            if "nsa" not in phases:
                return
            NQ = carve(0, 128, 4 * 2048, BF16)
            NQ3 = NQ.rearrange("p (g t) -> p g t", g=4)
            NKS = carve(16384, 128, 2048, BF16)
            NKW = carve(20480, 128, 2048, BF16)
            KC = carve(24576, 128, 128, BF16)
            KCT = carve(24832, 64, 2048, BF16)
            VCT = carve(28928, 64, 2048, BF16)
            VS = carve(33024, 128, NCH * 65, BF16).rearrange("p (c d) -> p c d", c=NCH)
            VW = carve(35104, 128, NCH * 65, BF16).rearrange("p (c d) -> p c d", c=NCH)
            ZN = carve(37184, 128, NCH * 256, BF16).rearrange("p (c d) -> p c d", c=NCH)
            GT = carve(45376, 128, NCH * 12, F32).rearrange("p (c d) -> p c d", c=NCH)
            W1S = carve(46144, 64, 2048, F32)
            W1B = [carve(54336, 64, 2048, BF16).rearrange("p (l e) -> p l e", l=32),
                   carve(58432, 64, 2048, BF16).rearrange("p (l e) -> p l e", l=32)]
            HID = [carve(62528, 64, 128, BF16), carve(62784, 64, 128, BF16)]
            W2S = carve(63040, 64, 64, F32)
            W2P = [carve(63296, 64, 128, BF16), carve(63552, 64, 128, BF16)]
            POS = carve(63808, 64, 32, F32)
            POSB = carve(63936, 64, 32, BF16)
            PBIAS = [carve(64000, 64, 1, F32), carve(64004, 64, 1, F32)]
            IMP = carve(64008, 128, 32, F32)
            MX8 = carve(64136, 128, 8, F32)
            SELM = carve(64168, 128, 32, F32)
            NEGB = carve(64296, 128, 32, BF16)
            DEN = carve(64360, 128, 12, F32)
            COEF = carve(64408, 128, 12, F32)
            ACC = carve(64456, 128, 64, F32)

            DMA(NQ[0:64, :], tab_d["nq_init"], w=["NQi"])
            DMA(NKS[0:64, :], tab_d["nks_init"], w=["NKSi"])
            DMA(NKW[0:64, :], tab_d["nkw_init"], w=["NKWi"])
            DMA(KC[0:64, :], tab_d["kc_init"], w=["KCi"])
            OP("pool", lambda e: e.memset(VS, 1.0), w=["VS1"])
            OP("pool", lambda e: e.memset(VW, 1.0), w=["VW1"])

            wb, wk = pref.pop("n1")
            for g in range(4):
                def ev(tg, pb, pk, g=g):
                    OP("act", lambda e, tg=tg, pb=pb, g=g: e.activation(out=NQ3[64:128, g, tg * 512:(tg + 1) * 512],
                                                                         in_=pb[64:128, 0:512], func=AF.Copy, scale=0.125),
                       r=[pk], w=[("NQ", g, tg)])
                proj_fm(wb, wk, 64 * g, 128, ev)

            def ev_ks(tg, pb, pk):
                OP("dve", lambda e, tg=tg, pb=pb: e.tensor_copy(out=NKS[64:128, tg * 512:(tg + 1) * 512], in_=pb[64:128, 0:512]),
                   r=[pk], w=[("NKS", tg)])

            def ev_kw(tg, pb, pk):
                OP("dve", lambda e, tg=tg, pb=pb: e.tensor_copy(out=NKW[64:128, tg * 512:(tg + 1) * 512], in_=pb[64:128, 0:512]),
                   r=[pk], w=[("NKW", tg)])

            def ev_kc(tg, pb, pk):
                OP("dve", lambda e, tg=tg, pb=pb: e.tensor_copy(out=KCT[:, tg * 512:(tg + 1) * 512], in_=pb[0:64, 0:512]),
                   r=[pk], w=[("KCT", tg)])

            def ev_vc(tg, pb, pk):
                OP("act", lambda e, tg=tg, pb=pb: e.activation(out=VCT[:, tg * 512:(tg + 1) * 512], in_=pb[0:64, 0:512], func=AF.Copy),
                   r=[pk], w=[("VCT", tg)])
            proj_fm(wb, wk, 320, 128, ev_ks)
            proj_fm(wb, wk, 384, 128, ev_kw)
            proj_fm(wb, wk, 0, 64, ev_kc)
            proj_fm(wb, wk, 320, 64, ev_vc)

            wb, wk = pref.pop("n2")

            def ev_n2(c, pb, pk):
                OP("dve", lambda e, c=c, pb=pb: e.tensor_copy(out=VS[:, c, 0:64], in_=pb[:, 0:64]), r=[pk, "VS1"], w=[("VS", c)])
                OP("dve", lambda e, c=c, pb=pb: e.tensor_copy(out=VW[:, c, 0:64], in_=pb[:, 64:128]), r=[pk, "VW1"], w=[("VW", c)])
                OP("dve", lambda e, c=c, pb=pb: e.tensor_copy(out=GT[:, c, :], in_=pb[:, 128:140]), r=[pk], w=[("GTraw", c)])
                OP("act", lambda e, c=c, pb=pb: e.activation(out=ZN[:, c, :], in_=pb[:, 140:396], func=AF.Silu),
                   r=[pk], w=[("ZN", c)])
            proj_tm(wb, wk, 0, 396, ev_n2)
            OP("act", lambda e: e.activation(out=GT, in_=GT, func=AF.Sigmoid), r=[("GTraw", c) for c in range(NCH)],
               w=[("GT", c) for c in range(NCH)])

            for kv, (w1_d, w2_d, pos_d, SRC, srck) in enumerate([(w1k_d, w2k_d, posk_d, KCT, "KCT"), (w1v_d, w2v_d, posv_d, VCT, "VCT")]):
                DMA(W1S, w1_d[l], w=["W1S"])
                DMA(W2S, w2_d[l], w=["W2S"])
                DMA(POS, pos_d[l], w=["POS"])
                OP("pool", lambda e, kv=kv: e.tensor_copy(out=W1B[kv].rearrange("p l e -> p (l e)"), in_=W1S), r=["W1S"], w=[("W1B", kv)])
                OP("pool", lambda e, kv=kv: e.tensor_copy(out=W2P[kv][:, 64:128], in_=W2S), r=["W2S"], w=[("W2P", kv)])
                OP("pool", lambda e: e.tensor_copy(out=POSB, in_=POS), r=["POS"], w=["POSB"])
                pb, pk = OB[kv]

                def fpb(e, kv=kv, pb=pb):
                    ins = None
                    for ll in range(32):
                        ins = e.matmul(pb[0:64, 0:1], lhsT=W1B[kv][:, ll, :], rhs=POSB[:, ll:ll + 1], start=(ll == 0), stop=(ll == 31))
                    return ins
                OP("pe", fpb, r=[("W1B", kv), "POSB"], w=[pk])
                OP("dve", lambda e, kv=kv, pb=pb: e.tensor_copy(out=PBIAS[kv], in_=pb[0:64, 0:1]), r=[pk], w=[("PBIAS", kv)])
                pb2, pk2 = OB[1 - kv]

                def fh(e, kv=kv, pb2=pb2, SRC=SRC):
                    ins = None
                    for ll in range(32):
                        ins = e.matmul(pb2[0:64, 0:127], lhsT=W1B[kv][:, ll, :], rhs=SRC[:, ll:ll + 16 * 126 + 1:16],
                                       start=(ll == 0), stop=(ll == 31))
                    return ins
                OP("pe", fh, r=[("W1B", kv)] + [(srck, tg) for tg in range(4)], w=[pk2])
                OP("act", lambda e, kv=kv, pb2=pb2: e.activation(out=HID[kv][:, 0:127], in_=pb2[0:64, 0:127], func=AF.Silu,
                                                                   bias=PBIAS[kv], scale=1.0),
                   r=[pk2, ("PBIAS", kv)], w=[("HID", kv)])
                if kv == 0:
                    OP("pool", lambda e: e.memset(W2P[0][:, 0:64], 0.0), w=["W2Pz"])
                    OP("pe", lambda e, pb=pb: e.matmul(pb[:, 0:127], lhsT=W2P[0][:, :], rhs=HID[0][:, 0:127], start=True, stop=True),
                       r=[("W2P", 0), "W2Pz", ("HID", 0), ("PBIAS", 0)], w=[pk])
                    OP("dve", lambda e, pb=pb: e.tensor_copy(out=KC[64:128, 0:127], in_=pb[64:128, 0:127]), r=[pk, "KCi"], w=["KC"])
                else:
                    OP("pe", lambda e, pb=pb: e.matmul(pb[0:127, 0:64], lhsT=HID[1][:, 0:127], rhs=W2P[1][:, 64:128], start=True, stop=True),
                       r=[("W2P", 1), ("HID", 1), ("PBIAS", 1)], w=[pk])
                    OP("dve", lambda e, pb=pb: e.tensor_copy(out=VC[0:127, 0:64], in_=pb[0:127, 0:64]), r=[pk, "VCc"], w=["VC"])

            wbs = []
            for half in range(2):
                wbs.append(wload(w_out3, [(512 * half, 512, 0)], scale_g=False))
            NQK = [("NQ", g, tg) for g in range(4) for tg in range(4)] + ["NQi"]
            tiles = [("cmp", 0, 0)]
            for i in range(NCH):
                if i + 1 < NCH:
                    tiles.append(("cmp", i + 1, 0))
                for cs in range(max(0, i - 4), i + 1):
                    tiles.append(("win", i, cs))
                for cs in range(i + 1):
                    tiles.append(("sel", i, cs))
            state = {}
            OCMP = (PB[:, 512:1024], "PB1")
            ST3 = [STB[0], STB[1], STB[2]]
            st3_rr = [0]

            def hook_sel(i):
                OP("pe", lambda e: e.transpose(out=PT1[0:32, 0:128], in_=NEGBs[i % 2], identity=IDENT[:]), r=[("NEGB", i % 2), "ident"], w=["PT1"])
                OP("dve", lambda e, i=i: e.tensor_copy(out=NQ3[0:32, :, i * 128:(i + 1) * 128],
                                                       in_=PT1[0:32, 0:128].unsqueeze(1).to_broadcast([32, 4, 128])),
                   r=["PT1"], w=[("NQsel", i)])

            def stA(n):
                kind, i, cs = tiles[n]
                if kind == "sel" and cs == 0:
                    hook_sel(i)
                st3_rr[0] = (st3_rr[0] + 1) % 3
                st, stk = ST3[st3_rr[0]]
                ptl, ptk = next_ptl()
                state[n] = (st, stk, ptl, ptk)
                rhs = NQ3[:, :, i * 128:(i + 1) * 128]
                if kind == "cmp":
                    ni = min(127, 8 * i + 7)

                    def f(e, st=st, rhs=rhs, i=i, ni=ni):
                        e.matmul(st[0:ni, 0:512], lhsT=KC[:, 0:ni], rhs=rhs, start=True, stop=False)
                        return e.matmul(st[0:ni, 0:512], lhsT=BIGBAND[:, 128 - 8 * i:128 - 8 * i + ni], rhs=MPAT[:, :],
                                        start=False, stop=True)
                    OP("pe", f, r=NQK + ["KC", "KCi", "bigband", "mpat", ("NQsel", i)], w=[stk])
                    OP("act", lambda e, st=st, ptl=ptl, ni=ni: e.activation(out=ptl[0:ni, :], in_=st[0:ni, 0:512], func=AF.Exp),
                       r=[stk], w=[ptk])
                else:
                    KT, kkeys = (NKW, [("NKW", tg) for tg in range(4)] + ["NKWi"]) if kind == "win" else \
                                (NKS, [("NKS", tg) for tg in range(4)] + ["NKSi"])
                    mask = None
                    if cs == i:
                        mask = MDLE4
                    elif kind == "win" and cs == i - 4:
                        mask = MDFAR4

                    def f(e, st=st, rhs=rhs, KT=KT, cs=cs, mask=mask):
                        ins = e.matmul(st[:, 0:512], lhsT=KT[:, cs * 128:(cs + 1) * 128], rhs=rhs, start=True, stop=(mask is None))
                        if mask is not None:
                            ins = e.matmul(st[:, 0:512], lhsT=IDENT[:], rhs=mask[:, :], start=False, stop=True)
                        return ins
                    OP("pe", f, r=NQK + kkeys + ["ident", "mdle4", "mdfar4", ("NQsel", i)], w=[stk])
                    OP("act", lambda e, st=st, ptl=ptl: e.activation(out=ptl[:, :], in_=st[:, 0:512], func=AF.Exp),
                       r=[stk], w=[ptk])

            def stC(n):
                kind, i, cs = tiles[n]
                st, stk, ptl, ptk = state.pop(n)
                if kind == "cmp":
                    ni = min(127, 8 * i + 7)
                    ob, obk = OCMP

                    def f(e, ptl=ptl, ni=ni, ob=ob):
                        ins = None
                        for g in range(4):
                            ins = e.matmul(ob[:, g * 97:(g + 1) * 97], lhsT=ptl[0:ni, g * 128:(g + 1) * 128], rhs=VC[0:ni, :],
                                           start=(g == 0), stop=True, skip_group_check=True)
                        return ins
                    OP("pe", f, r=[ptk, "VC", "VCc"], w=[obk])
                    ob3 = ob[:, 0:388].rearrange("p (g d) -> p g d", g=4)
                    OP("dve", lambda e, ob3=ob3: e.tensor_scalar(out=DENs[i % 2][:, 0:4], in0=ob3[:, :, 64], scalar1=1e-30, scalar2=None, op0=ALU.max),
                       r=[obk], w=[("DENc", i % 2)])
                    OP("dve", lambda e: e.reciprocal(out=DENs[i % 2][:, 0:4], in_=DENs[i % 2][:, 0:4]), r=[("DENc", i % 2)], w=[("RDc", i % 2)])
                    for g in range(4):
                        if g == 0:
                            OP("dve", lambda e, ob3=ob3: e.tensor_scalar(out=IMP, in0=ob3[:, 0, 65:97], scalar1=DENs[i % 2][:, 0:1], scalar2=None,
                                                                         op0=ALU.mult), r=[obk, ("RDc", i % 2)], w=["IMP"])
                        else:
                            OP("dve", lambda e, ob3=ob3, g=g: e.scalar_tensor_tensor(out=IMP, in0=ob3[:, g, 65:97], scalar=DENs[i % 2][:, g:g + 1],
                                                                                   in1=IMP, op0=ALU.mult, op1=ALU.add),
                               r=[obk, ("RDc", i % 2), "IMP"], w=["IMP"])
                    OP("dve", lambda e, i=i: e.tensor_tensor(out=IMP, in0=IMP, in1=ADDM[:, i, :], op=ALU.add), r=["IMP", "addm"], w=["IMP"])
                    OP("dve", lambda e: e.max(out=MX8, in_=IMP), r=["IMP"], w=["MX8"])
                    OP("dve", lambda e: e.tensor_scalar(out=SELM, in0=IMP, scalar1=MX8[:, 7:8], scalar2=None, op0=ALU.is_ge),
                       r=["IMP", "MX8"], w=["SELM"])
                    OP("dve", lambda e: e.tensor_scalar(out=NEGBs[i % 2], in0=SELM, scalar1=-NEG, scalar2=NEG, op0=ALU.mult, op1=ALU.add),
                       r=["SELM"], w=[("NEGB", i % 2)])
                    OP("dve", lambda e, i=i: e.tensor_tensor(out=COEFs[i % 2][:, 0:4], in0=DENs[i % 2][:, 0:4], in1=GT[:, i, 0:12:3], op=ALU.mult),
                       r=[("RDc", i % 2), ("GT", i)], w=[("COEFc", i % 2)])
                    for g in range(4):
                        OP("dve", lambda e, ob3=ob3, g=g, i=i: e.tensor_scalar(
                            out=OACCs[i % 2][:, g, :], in0=ob3[:, g, 0:64], scalar1=COEFs[i % 2][:, g:g + 1], scalar2=None, op0=ALU.mult),
                           r=[obk, ("COEFc", i % 2), ("OACC", g, i % 2)], w=[("OACC", g, i % 2)])
                    return
                VT, vk = (VW, "VW") if kind == "win" else (VS, "VS")
                ob, obk = OB[0] if kind == "sel" else OB[1]
                first = (cs == (0 if kind == "sel" else max(0, i - 4)))
                lastt = (cs == i)

                def f(e, ptl=ptl, ob=ob, VT=VT, cs=cs, first=first, lastt=lastt):
                    ins = None
                    for g in range(4):
                        ins = e.matmul(ob[:, g * 65:(g + 1) * 65], lhsT=ptl[:, g * 128:(g + 1) * 128], rhs=VT[:, cs, :],
                                       start=(first and g == 0), stop=lastt, skip_group_check=True)
                    return ins
                OP("pe", f, r=[ptk, (vk, cs)], w=[obk])
                if lastt:
                    ob3 = ob[:, 0:260].rearrange("p (g d) -> p g d", g=4)
                    bi = 1 if kind == "sel" else 2
                    dk = ("DEN", bi, i % 2)
                    OP("dve", lambda e, ob3=ob3, bi=bi: e.reciprocal(out=DENs[i % 2][:, 4 * bi:4 * bi + 4], in_=ob3[:, :, 64]), r=[obk], w=[dk])
                    OP("dve", lambda e, bi=bi, i=i: e.tensor_tensor(out=COEFs[i % 2][:, 4 * bi:4 * bi + 4], in0=DENs[i % 2][:, 4 * bi:4 * bi + 4],
                                                                    in1=GT[:, i, bi:12:3], op=ALU.mult),
                       r=[dk, ("GT", i)], w=[("COEF", bi, i % 2)])
                    for g in range(4):
                        OP("dve", lambda e, ob3=ob3, g=g, bi=bi: e.scalar_tensor_tensor(
                            out=OACCs[i % 2][:, g, :], in0=ob3[:, g, 0:64], scalar=COEFs[i % 2][:, 4 * bi + g:4 * bi + g + 1], in1=OACCs[i % 2][:, g, :],
                            op0=ALU.mult, op1=ALU.add), r=[obk, ("COEF", bi, i % 2), ("OACC", g, i % 2)], w=[("OACC", g, i % 2)])
                    if kind == "sel":
                        OP("dve", lambda e, i=i: e.tensor_tensor(
                            out=M[:, i, 768:1024].rearrange("p (g d) -> p g d", g=4), in0=OACCs[i % 2][:, :, :],
                            in1=ZN[:, i, :].rearrange("p (g d) -> p g d", g=4), op=ALU.mult),
                           r=[("OACC", g, i % 2) for g in range(4)] + [("ZN", i)], w=[("M", i, 12)])

            OACCs = [SMALL[:, 0:256].rearrange("p (g d) -> p g d", g=4),
                     carve(64712, 128, 256, F32).rearrange("p (g d) -> p g d", g=4)]
            NEGBs = [NEGB, carve(65736, 128, 32, BF16)]
            DENs = [DEN, carve(65800, 128, 12, F32)]
            COEFs = [COEF, carve(65848, 128, 12, F32)]
            DEPTH = 3
            next_c = [0]
            cmp_idx = {}
            for n_, (kind_, i_, cs_) in enumerate(tiles):
                if kind_ == "cmp":
                    cmp_idx[i_] = n_
            for m in range(len(tiles) + DEPTH):
                while next_c[0] <= m - DEPTH:
                    stC(next_c[0])
                    next_c[0] += 1
                if m < len(tiles):
                    kind_, i_, cs_ = tiles[m]
                    if kind_ == "sel" and cs_ == 0:
                        while next_c[0] <= cmp_idx[i_]:
                            stC(next_c[0])
                            next_c[0] += 1
                    stA(m)
            S.barrier()

            if "out" not in phases:
                return
            for c in range(NCH):
                transpose_into_HT(M[:, c, :], [("M", c, j) for j in range(13)], c)
            evs = {}

            def o_dma(n):
                b = n % 2
                DMA(XT[b][:], xsrc(n), r=[("XS", n)], w=["XT%d" % b])
                if nxt == "seq":
                    DMA(XT[2 + b][:], x_d[s + 1][n * 128:(n + 1) * 128, :], w=["XT%d" % (2 + b)])

            def o_mm(n):
                pacc, pk0, pk1 = (PA, "PA0", "PA1") if n % 2 == 0 else (PB, "PB0", "PB1")

                def f(e, c=n, pacc=pacc):
                    ins = None
                    for half in range(2):
                        wb = wbs[half][0]
                        for k in range(8):
                            ins = e.matmul(pacc[:, half * 512:(half + 1) * 512], lhsT=HT[:, k, c * 128:(c + 1) * 128],
                                           rhs=wb[:, k, 0:512], start=(k == 0), stop=(k == 7))
                    return ins
                OP("pe", f, r=[("HT", n)] + wbs[0][1] + wbs[1][1], w=[pk0, pk1])

            def o_chain(n):
                b = n % 2
                c = n
                pacc, pk0, pk1 = (PA, "PA0", "PA1") if n % 2 == 0 else (PB, "PB0", "PB1")
                OP("dve", lambda e, b=b, pacc=pacc: e.tensor_tensor(out=XT[b][:], in0=XT[b][:], in1=pacc[:, :], op=ALU.add),
                   r=["XT%d" % b, pk0, pk1], w=["XT%d" % b])
                if not last:
                    DMA(xs_d[c * 128:(c + 1) * 128, :], XT[b][:], r=["XT%d" % b], w=[("XS", c)])
                    if nxt == "layer":
                        xn, xnk = stats_scale(XT[b], "XT%d" % b, c, n)
                        evs[n] = tpose(xn, [xnk], n)
                else:
                    OP("dve", lambda e, b=b, c=c: e.scalar_tensor_tensor(
                        out=XN[2][:], in0=XT[b][:], scalar=1.0, in1=XT[b][:], op0=ALU.mult, op1=ALU.mult,
                        accum_out=SS[:, c:c + 1]), r=["XT%d" % b], w=["XJ", ("SS", c)])
                    OP("act", lambda e, c=c: e.activation(out=SQ_[:, c:c + 1], in_=SS[:, c:c + 1], func=AF.Sqrt,
                                                          bias=EPS, scale=1.0 / D), r=[("SS", c)], w=[("SQ", c)])
                    OP("dve", lambda e, c=c: e.reciprocal(out=RS[:, c:c + 1], in_=SQ_[:, c:c + 1]), r=[("SQ", c)], w=[("RS", c)])
                    OP("dve", lambda e, b=b, c=c: e.scalar_tensor_tensor(
                        out=XT[b][:], in0=XT[b][:], scalar=RS[:, c:c + 1], in1=FG[:], op0=ALU.mult, op1=ALU.mult),
                       r=["XT%d" % b, ("RS", c), "FG"], w=["XT%d" % b])
                    DMA(y_d[s][c * 128:(c + 1) * 128, :], XT[b][:], r=["XT%d" % b], w=[("Y", s, c)])
                    if nxt == "seq":
                        xn, xnk = stats_scale(XT[2 + b], "XT%d" % (2 + b), NCH + c, n)
                        evs[n] = tpose(xn, [xnk], n)

            def o_evac(n):
                if n in evs:
                    evs.pop(n)()
            run_stages([(o_mm, 0), (o_chain, 1), (o_dma, -1), (o_evac, 2)], 0, NCH)
            if nxt == "layer":
                prefetch_next(l + 1)
            elif nxt == "seq":
                prefetch_next(0)

        for s in range(n_seq):
            for l in range(depth):
                nxt = "layer" if l < depth - 1 else ("seq" if s < n_seq - 1 else None)
                unit(s, l, l == depth - 1, s == 0 and l == 0, nxt)
        OP("sp", lambda e: e.nop(), r=[("Y", s, c) for s in range(n_seq) for c in range(NCH)])
        S.analyze()
        S.emit(nc, sems, dsems)
    return nc


def _host_layout(inputs, tables):
    f = lambda a: np.ascontiguousarray(np.asarray(a, dtype=np.float32))
    norm_g = f(inputs["norm_g"])
    common = {
        "w_in": f(inputs["w_in"]),
        "w_out": f(inputs["w_out"]),
        "normg_t": np.ascontiguousarray(norm_g.reshape(2, 8, 128).transpose(0, 2, 1)),
        "bf3": np.ascontiguousarray(np.repeat(f(inputs["b_f"]), 3, axis=1).reshape(2, 18, 1)),
        "fg_rep": np.ascontiguousarray(np.broadcast_to(f(inputs["final_g"])[None, :], (128, D))),
        "posk_t": np.ascontiguousarray(f(inputs["cmp_pos_k"]).transpose(0, 2, 1)),
        "posv_t": np.ascontiguousarray(f(inputs["cmp_pos_v"]).transpose(0, 2, 1)),
        "w1k_t": np.ascontiguousarray(f(inputs["cmp_w1_k"]).reshape(2, 32, 64, 64).transpose(0, 2, 1, 3).reshape(2, 64, 2048)),
        "w1v_t": np.ascontiguousarray(f(inputs["cmp_w1_v"]).reshape(2, 32, 64, 64).transpose(0, 2, 1, 3).reshape(2, 64, 2048)),
        "w2k": f(inputs["cmp_w2_k"]),
        "w2v": f(inputs["cmp_w2_v"]),
    }
    for k, v in tables.items():
        common["t_" + k] = np.ascontiguousarray(v)
    return common


def kernel(x, norm_g, w_in, b_f, cmp_pos_k, cmp_w1_k, cmp_w2_k, cmp_pos_v, cmp_w1_v, cmp_w2_v, w_out, final_g):
    inputs = dict(x=x, norm_g=norm_g, w_in=w_in, b_f=b_f, cmp_pos_k=cmp_pos_k, cmp_w1_k=cmp_w1_k, cmp_w2_k=cmp_w2_k,
                  cmp_pos_v=cmp_pos_v, cmp_w1_v=cmp_w1_v, cmp_w2_v=cmp_w2_v, w_out=w_out, final_g=final_g)
    n_cores = 8
    tables = _tables()
    common = _host_layout(inputs, tables)
    xs = np.ascontiguousarray(np.asarray(x, dtype=np.float32))
    per = xs.shape[0] // n_cores
    nc = build_nc(n_seq=per, depth=2, tables=tables)
    in_maps = []
    for c in range(n_cores):
        m = dict(common)
        m["x"] = xs[c * per:(c + 1) * per]
        in_maps.append(m)
    res = run_bass_kernel_spmd(nc, in_maps, core_ids=list(range(n_cores)))
    return np.concatenate([np.asarray(r["y"], dtype=np.float32) for r in res.results], axis=0)
```
